# Optimizing a Trainium2 kernel written in Bass

```python
import math
import jax, jax.numpy as jnp
from jax import lax
import numpy as np

D_MODEL = 1024
BATCH = 4
SEQ = 8192
DEPTH = 2

HEAD_DIM = 64
A_GROUPS = 4
A_WIDTH = A_GROUPS * HEAD_DIM
A_CHUNK = 128
B_Q_HEADS = 8
B_KV_HEADS = 2
B_GROUP = B_Q_HEADS // B_KV_HEADS
B_WIDTH = B_Q_HEADS * HEAD_DIM
B_KV_WIDTH = B_KV_HEADS * HEAD_DIM
WINDOW = 128
REL_BUCKETS = 32
REL_MAX_DIST = 128
C_HEADS = 4
C_KEY_DIM = 64
C_VAL_DIM = 64
C_KEY_WIDTH = C_HEADS * C_KEY_DIM
C_WIDTH = C_HEADS * C_VAL_DIM
C_CHUNK = 16
MIX_WIDTH = A_WIDTH + B_WIDTH + C_WIDTH
IN_WIDTH = 2 * A_WIDTH + B_WIDTH + 2 * B_KV_WIDTH + 2 * C_KEY_WIDTH + 2 * C_WIDTH
D_FF = 2816
CONV_WIDTH = 3
EPS = 1e-6
MASK_VALUE = -1e30

kernel_name = "hybrid_gmlp_swa_hgrn2_block"


def rms_norm(x, g):
    xf = x.astype(jnp.float32)
    y = xf * lax.rsqrt(jnp.mean(xf * xf, axis=-1, keepdims=True) + EPS)
    return (y * g.astype(jnp.float32)).astype(x.dtype)


def spatial_gating_mixer(u, v, vnorm_g, w_s, b_s):
    B, S = u.shape[:2]
    n_chunks = S // A_CHUNK
    v = rms_norm(v, vnorm_g)
    causal = jnp.tril(jnp.ones((A_CHUNK, A_CHUNK), dtype=bool))
    w = jnp.where(causal, w_s, jnp.zeros_like(w_s))
    vc = v.reshape(B, n_chunks, A_CHUNK, A_GROUPS, HEAD_DIM)
    sv = jnp.einsum('gts,bcsgd->bctgd', w, vc) + b_s.T[:, :, None]
    return (u.reshape(vc.shape) * sv).reshape(B, S, A_WIDTH)


def t5_causal_bucket(dist):
    max_exact = REL_BUCKETS // 2
    n = jnp.maximum(dist, 0)
    is_small = n < max_exact
    nf = jnp.maximum(n, 1).astype(jnp.float32)
    large = max_exact + (jnp.log(nf / max_exact) / math.log(REL_MAX_DIST / max_exact)
                         * (REL_BUCKETS - max_exact)).astype(jnp.int32)
    large = jnp.minimum(large, REL_BUCKETS - 1)
    return jnp.where(is_small, n, large)


def sliding_window_attention(q, k, v, qn_g, kn_g, sinks, rel_bias):
    f32 = jnp.float32
    B, S = q.shape[:2]
    nb = S // WINDOW
    q = rms_norm(q, qn_g).astype(f32)
    k = rms_norm(k, kn_g).astype(f32)
    v = v.astype(f32)
    qb = q.reshape(B, nb, WINDOW, B_KV_HEADS, B_GROUP, HEAD_DIM)

    def banded(t):
        tb = t.reshape(B, nb, WINDOW, B_KV_HEADS, HEAD_DIM)
        prev = jnp.concatenate([jnp.zeros_like(tb[:, :1]), tb[:, :-1]], axis=1)
        return jnp.concatenate([prev, tb], axis=2)

    kw, vw = banded(k), banded(v)
    scores = jnp.einsum('bnihgd,bnjhd->bhgnij', qb, kw) * (HEAD_DIM ** -0.5)
    qi = jnp.arange(WINDOW)[:, None]
    kj = jnp.arange(2 * WINDOW)[None, :]
    dist = qi + WINDOW - kj
    bias = rel_bias.astype(f32)[t5_causal_bucket(dist)]
    bias = bias.reshape(WINDOW, 2 * WINDOW, B_KV_HEADS, B_GROUP).transpose(2, 3, 0, 1)[:, :, None]
    blk = jnp.arange(nb)[:, None, None]
    valid = (dist >= 0) & (dist < WINDOW) & (blk * WINDOW - WINDOW + kj >= 0)
    scores = jnp.where(valid, scores + bias, MASK_VALUE)
    sink = sinks.astype(f32).reshape(B_KV_HEADS, B_GROUP, 1, 1, 1)
    m = jnp.maximum(scores.max(axis=-1, keepdims=True), sink)
    p = jnp.exp(scores - m)
    denom = p.sum(axis=-1, keepdims=True) + jnp.exp(sink - m)
    o = jnp.einsum('bhgnij,bnjhd->bnihgd', p / denom, vw)
    return o.reshape(B, S, B_WIDTH)


def hgrn2_mixer(q, fz, i, g, lb, onorm_g):
    f32 = jnp.float32
    B, S = q.shape[:2]
    N = S // C_CHUNK
    lb = lb.astype(f32)
    z = fz.astype(f32)
    q = jax.nn.silu(q.astype(f32))
    log_f = jnp.logaddexp(jnp.log(lb), jnp.log1p(-lb) + jax.nn.log_sigmoid(z))
    k = (1.0 - lb) * jax.nn.sigmoid(-z)
    shp = (B, N, C_CHUNK, C_HEADS, C_KEY_DIM)
    q, k, log_f = q.reshape(shp), k.reshape(shp), log_f.reshape(shp)
    v = i.astype(f32).reshape(B, N, C_CHUNK, C_HEADS, C_VAL_DIM)
    cum = jnp.cumsum(log_f, axis=2)
    causal = jnp.tril(jnp.ones((C_CHUNK, C_CHUNK), dtype=bool))[:, :, None, None]
    rel = jnp.exp(jnp.where(causal, cum[:, :, :, None] - cum[:, :, None, :], -jnp.inf))
    attn = jnp.einsum('bnthk,bntshk,bnshk->bnhts', q, rel, k)
    o = jnp.einsum('bnhts,bnshv->bnthv', attn, v)
    last = cum[:, :, -1]
    w_k = k * jnp.exp(last[:, :, None] - cum)
    dstate = jnp.einsum('bnshk,bnshv->nbhkv', w_k, v)
    decay = jnp.exp(last).transpose(1, 0, 2, 3)

    def step(state, inp):
        dec, ds = inp
        return dec[..., None] * state + ds, state

    init = jnp.zeros((B, C_HEADS, C_KEY_DIM, C_VAL_DIM), f32)
    _, s_in = lax.scan(step, init, (decay, dstate))
    o = o + jnp.einsum('bnthk,nbhkv->bnthv', q * jnp.exp(cum), s_in)
    o = o.reshape(B, S, C_HEADS, C_VAL_DIM)
    o = rms_norm(o, onorm_g) * jax.nn.silu(g.astype(f32))
    return o.reshape(B, S, C_WIDTH)


def conv_ffn(h, w_gate, w_up, conv_w, conv_b, w_down):
    S = h.shape[1]
    gate = h @ w_gate
    gp = jnp.pad(gate, ((0, 0), (CONV_WIDTH - 1, 0), (0, 0)))
    conv = conv_b
    for tap in range(CONV_WIDTH):
        conv = conv + conv_w[tap] * gp[:, tap:tap + S]
    return (jax.nn.silu(conv) * (h @ w_up)) @ w_down


def setup_inputs(seed: int = 0) -> dict:
    key = jax.random.key(seed)
    ks = jax.random.split(key, 20)
    nrm = jax.random.normal
    f32 = jnp.float32
    return {
        "x": nrm(ks[0], (BATCH, SEQ, D_MODEL), f32),
        "norm1_g": 1.0 + 0.02 * nrm(ks[1], (DEPTH, D_MODEL), f32),
        "w_in": nrm(ks[2], (DEPTH, D_MODEL, IN_WIDTH), f32) * D_MODEL ** -0.5,
        "gmlp_vnorm_g": 1.0 + 0.02 * nrm(ks[3], (DEPTH, A_GROUPS, HEAD_DIM), f32),
        "gmlp_w_s": nrm(ks[4], (DEPTH, A_GROUPS, A_CHUNK, A_CHUNK), f32) * A_CHUNK ** -0.5,
        "gmlp_b_s": 1.0 + 0.02 * nrm(ks[5], (DEPTH, A_GROUPS, A_CHUNK), f32),
        "q_norm_g": 1.0 + 0.02 * nrm(ks[6], (DEPTH, HEAD_DIM), f32),
        "k_norm_g": 1.0 + 0.02 * nrm(ks[7], (DEPTH, HEAD_DIM), f32),
        "attn_sinks": 0.5 * nrm(ks[8], (DEPTH, B_Q_HEADS), f32),
        "rel_bias": 0.5 * nrm(ks[9], (REL_BUCKETS, B_Q_HEADS), f32),
        "hgrn_lb_logits": nrm(ks[10], (DEPTH, C_KEY_WIDTH), f32),
        "hgrn_onorm_g": 1.0 + 0.02 * nrm(ks[11], (DEPTH, C_VAL_DIM), f32),
        "w_out": nrm(ks[12], (DEPTH, MIX_WIDTH, D_MODEL), f32) * MIX_WIDTH ** -0.5,
        "norm2_g": 1.0 + 0.02 * nrm(ks[13], (DEPTH, D_MODEL), f32),
        "w_gate": nrm(ks[14], (DEPTH, D_MODEL, D_FF), f32) * D_MODEL ** -0.5,
        "w_up": nrm(ks[15], (DEPTH, D_MODEL, D_FF), f32) * D_MODEL ** -0.5,
        "conv_w": nrm(ks[16], (DEPTH, CONV_WIDTH, D_FF), f32) * CONV_WIDTH ** -0.5,
        "conv_b": 0.02 * nrm(ks[17], (DEPTH, D_FF), f32),
        "w_down": nrm(ks[18], (DEPTH, D_FF, D_MODEL), f32) * D_FF ** -0.5,
    }


def reference(x, norm1_g, w_in, gmlp_vnorm_g, gmlp_w_s, gmlp_b_s, q_norm_g, k_norm_g,
              attn_sinks, rel_bias, hgrn_lb_logits, hgrn_onorm_g, w_out, norm2_g,
              w_gate, w_up, conv_w, conv_b, w_down):
    B, S, _ = x.shape
    widths = [A_WIDTH, A_WIDTH, B_WIDTH, B_KV_WIDTH, B_KV_WIDTH,
              C_KEY_WIDTH, C_KEY_WIDTH, C_WIDTH, C_WIDTH]
    split_idx = np.cumsum(widths)[:-1].tolist()
    lb_cum = jnp.cumsum(jax.nn.softmax(hgrn_lb_logits.astype(jnp.float32), axis=0), axis=0)
    lower_bounds = lb_cum - lb_cum[0]
    for l in range(DEPTH):
        h = rms_norm(x, norm1_g[l])
        proj = h @ w_in[l]
        a_u, a_v, b_q, b_k, b_v, c_q, c_f, c_i, c_g = jnp.split(proj, split_idx, axis=-1)
        u = jax.nn.gelu(a_u, approximate=False).reshape(B, S, A_GROUPS, HEAD_DIM)
        v = jax.nn.gelu(a_v, approximate=False).reshape(B, S, A_GROUPS, HEAD_DIM)
        y_a = spatial_gating_mixer(u, v, gmlp_vnorm_g[l], gmlp_w_s[l], gmlp_b_s[l])
        y_b = sliding_window_attention(
            b_q.reshape(B, S, B_Q_HEADS, HEAD_DIM),
            b_k.reshape(B, S, B_KV_HEADS, HEAD_DIM),
            b_v.reshape(B, S, B_KV_HEADS, HEAD_DIM),
            q_norm_g[l], k_norm_g[l], attn_sinks[l], rel_bias)
        y_c = hgrn2_mixer(
            c_q.reshape(B, S, C_HEADS, C_KEY_DIM),
            c_f.reshape(B, S, C_HEADS, C_KEY_DIM),
            c_i.reshape(B, S, C_HEADS, C_VAL_DIM),
            c_g.reshape(B, S, C_HEADS, C_VAL_DIM),
            lower_bounds[l].reshape(C_HEADS, C_KEY_DIM), hgrn_onorm_g[l])
        mixed = jnp.concatenate([y_a.astype(x.dtype), y_b.astype(x.dtype), y_c.astype(x.dtype)], axis=-1)
        x = x + mixed @ w_out[l]
        h = rms_norm(x, norm2_g[l])
        x = x + conv_ffn(h, w_gate[l], w_up[l], conv_w[l], conv_b[l], w_down[l])
    return x
```

```python
import contextlib
import numpy as np
import concourse.bass as bass
import concourse.mybir as mybir
from concourse.bass_utils import run_bass_kernel_spmd

F32 = mybir.dt.float32
BF16 = mybir.dt.bfloat16
AF = mybir.ActivationFunctionType
ALU = mybir.AluOpType
AX = mybir.AxisListType

D = 1024
DFF = 2816
NCH = 22
NL = 2
TT = 512
NB = 4
EPS = 1e-6
SOFT_C = 0.0
NDMASEM = 12
NRING = 4
NTM = 1152
HCH = 32
HN = 128 // HCH
NFM = 1280


class T:
    __slots__ = ("name", "w", "r")

    def __init__(self, name):
        self.name = name
        self.w = {}
        self.r = {}


class Prog:
    def __init__(self, nc, stack):
        self.nc = nc
        self.st = stack
        self.eng = {"pe": nc.tensor, "act": nc.scalar, "dve": nc.vector,
                    "pool": nc.gpsimd, "sp": nc.sync}
        self.sem = {e: stack.enter_context(nc.semaphore("s_" + e)) for e in self.eng}
        self.cnt = {e: 0 for e in self.eng}
        self.dsem = {}
        self.dcnt = {}
        self.nds = {"sp": NDMASEM, "pool": 80, "act": 1}
        for q in ("sp", "pool", "act"):
            self.dsem[q] = [stack.enter_context(nc.semaphore("d_%s%d" % (q, i)))
                            for i in range(self.nds[q])]
            self.dcnt[q] = 0
        self.seen = {e: {} for e in self.eng}
        self.nwait = 0
        self.nops = 0
        self.marks = []

    def mark(self, name):
        self.marks.append((name, dict(self.cnt)))

    def sb(self, name, shape, dt=F32):
        return self.st.enter_context(self.nc.sbuf_tensor("sb_" + name, list(shape), dt))

    def _wait(self, e, tok, raw=False):
        kind, key, val = tok
        if kind == "eng":
            if key == e and not (raw and e != "pe"):
                return
            sem = self.sem[key]
            sk = "e_" + key
        else:
            q, i = key
            sem = self.dsem[q][i]
            sk = "d_%s%d" % (q, i)
        if self.seen[e].get(sk, 0) >= val:
            return
        self.eng[e].wait_ge(sem, val)
        self.seen[e][sk] = val
        self.nwait += 1

    def _deps(self, e, reads, writes, add):
        for t in reads:
            for tok in t.w.values():
                self._wait(e, tok, raw=True)
        for t in writes:
            if not add:
                for tok in t.w.values():
                    self._wait(e, tok, raw=True)
            for tok in t.r.values():
                self._wait(e, tok, raw=True)

    def _mark(self, tok, reads, writes, add):
        for t in reads:
            t.r[tok[1]] = tok
        for t in writes:
            if add:
                t.w[tok[1]] = tok
            else:
                t.w = {tok[1]: tok}
            t.r = {}

    def op(self, e, fn, reads=(), writes=()):
        self._deps(e, reads, writes, False)
        ins = fn(self.eng[e])
        self.cnt[e] += 1
        ins.then_inc(self.sem[e], 1)
        self._mark(("eng", e, self.cnt[e]), reads, writes, False)
        self.nops += 1
        return ins

    def dma(self, q, out, in_, reads=(), writes=(), add=False, **kw):
        n = self.dcnt[q]
        slot = n % self.nds[q]
        gen = n // self.nds[q]
        if gen > 0:
            self._wait(q, ("dma", (q, slot), 16 * gen))
        self._deps(q, reads, writes, add)
        ins = self.eng[q].dma_start(out=out, in_=in_, **kw)
        ins.then_inc(self.dsem[q][slot], 16)
        self.dcnt[q] = n + 1
        self._mark(("dma", (q, slot), 16 * (gen + 1)), reads, writes, add)
        return ins

    def drain(self, e, queues=("sp", "pool", "act")):
        for q in queues:
            n = self.dcnt[q]
            for slot in range(self.nds[q]):
                if n > slot:
                    k = (n - 1 - slot) // self.nds[q] + 1
                    self._wait(e, ("dma", (q, slot), 16 * k))


class _Stop(Exception):
    pass


def build(ntok, nl=NL, stop=None, debug=False):
    assert ntok % TT == 0
    ntile = ntok // TT
    nc = bass.Bass("TRN2", target_bir_lowering=False)

    def din(name, shape, dt=F32):
        return nc.dram_tensor(name, list(shape), dt, kind="ExternalInput").ap()

    x_d = din("x", [ntok, D])
    win_d = din("w_in_r", [nl, D, NTM + NFM])
    wout_d = din("w_out", [nl, D, D])
    wg_d = din("w_gate", [nl, D, DFF])
    wu_d = din("w_up", [nl, D, DFF])
    wd_d = din("w_down", [nl, DFF, D])
    colpar_d = din("colpar", [128, nl, 110])
    rowpar_d = din("rowpar", [nl, 520])
    wsT_d = din("wsT", [nl, 128, 4, 128])
    bT_d = din("bT", [nl, 128, 4])
    biasT_d = din("biasT", [128, 2, 8, 128])
    cst_d = din("cst", [128, 3, 128])
    cm_d = din("cm", [128, HN])
    blk2_d = din("blk2", [128, 2])
    blk2T_d = din("blk2T", [2, 128])
    out_d = nc.dram_tensor("out", [ntok, D], F32, kind="ExternalOutput").ap()

    def dscr(name, shape):
        return nc.dram_tensor(name, list(shape), BF16, kind="Internal").ap()

    winfm_s = dscr("winfm_s", [nl, 128, 8, NFM])
    wintm_s = dscr("wintm_s", [nl, 128, 8, NTM])
    wout_s = dscr("wout_s", [nl, 2, 128, 8, 512])
    wg_s = dscr("wg_s", [nl, 11, 128, 8, 256])
    wu_s = dscr("wu_s", [nl, 11, 128, 8, 256])
    wd_s = dscr("wd_s", [nl, 6, 128, 4, 1024])

    with contextlib.ExitStack() as st:
        P = Prog(nc, st)
        sb = P.sb

        ident = sb("ident", [128, 128], BF16); t_ident = T("ident")
        cst = sb("cst", [128, 3, 128]); t_cst = T("cst")
        cm = sb("cm", [128, HN]); t_cm = T("cm")
        blk2 = sb("blk2", [128, 2]); blk2T = sb("blk2T", [2, 128]); t_blk = T("blk")
        colpar = sb("colpar", [128, nl, 110]); t_colpar = T("colpar")
        rowpar = sb("rowpar", [128, nl, 520]); t_rowpar = T("rowpar")
        esink = sb("esink", [128, nl, 8]); t_esink = T("esink")
        wsT = sb("wsT", [128, nl, 4, 128], BF16); t_wsT = T("wsT")
        bT = sb("bT", [128, nl, 4]); t_bT = T("bT")
        bias8 = sb("bias8", [128, 2, 2, 2, 2, 128], BF16); t_bias8 = T("bias8")
        lbp = sb("lbp", [128, nl, 2, 2]); t_lbp = T("lbp")

        cqf = sb("cqf", [128, 4, TT]); t_cqf = T("cqf")
        stage = cqf[:].rearrange("p a t -> p (a t)").rearrange("p (a b c) -> p a b c", a=2, b=8)
        t_stage = t_cqf
        P.op("pool", lambda e: e.memset(ident[:], 0.0), writes=[t_ident])
        P.op("pool", lambda e: e.affine_select(out=ident[:], in_=ident[:], pattern=[[-1, 128]],
                                               compare_op=ALU.not_equal, fill=1.0, base=0,
                                               channel_multiplier=1),
             reads=[t_ident], writes=[t_ident])
        P.dma("sp", cst[:], cst_d, writes=[t_cst])
        P.dma("sp", cm[:], cm_d, writes=[t_cm])
        P.dma("sp", blk2[:], blk2_d, writes=[t_blk])
        P.dma("sp", blk2T[:], blk2T_d, writes=[t_blk], add=True)
        P.dma("sp", colpar[:], colpar_d, writes=[t_colpar])
        for l in range(nl):
            P.dma("sp", rowpar[:, l, :], rowpar_d[l].partition_broadcast(128),
                  writes=[t_rowpar], add=(l > 0))
        P.dma("sp", bT[:], bT_d.rearrange("l p g -> p l g"), writes=[t_bT])
        for l in range(nl):
            P.dma("sp", stage[:, 0, 0:4, :], wsT_d[l], writes=[t_stage])
            P.op("dve", lambda e: e.tensor_tensor(
                out=wsT[:, l, :, :], in0=stage[:, 0, 0:4, :],
                in1=cst[:, 0, :].unsqueeze(1).to_broadcast([128, 4, 128]), op=ALU.mult),
                reads=[t_stage, t_cst], writes=[t_wsT])
        P.dma("sp", stage, biasT_d, writes=[t_stage])
        for g in range(2):
            for e2 in range(2):
                P.op("dve", lambda e: e.tensor_scalar(out=bias8[:, g, e2, :, :, :],
                                                      in0=stage[:, :, 4 * g + e2:4 * g + 4:2, :],
                                                      scalar1=8.0, scalar2=None, op0=ALU.mult),
                     reads=[t_stage], writes=[t_bias8])
        for l in range(nl):
            P.op("act", lambda e: e.activation(out=esink[:, l, :], in_=rowpar[:, l, 512:520],
                                               func=AF.Exp),
                 reads=[t_rowpar], writes=[t_esink])
        P.op("dve", lambda e: e.tensor_scalar(out=esink[:], in0=esink[:], scalar1=float(np.exp(-SOFT_C)),
                                              scalar2=None, op0=ALU.mult),
             reads=[t_esink], writes=[t_esink])
        ltmp = sb("ltmp", [128, 8]); t_ltmp = T("ltmp")
        if nl == 2:
            P.op("dve", lambda e: e.tensor_tensor(out=ltmp[:, 0:2], in0=colpar[:, 0, 20:22],
                                                  in1=colpar[:, 0, 18:20], op=ALU.subtract),
                 reads=[t_colpar], writes=[t_ltmp])
            P.op("act", lambda e: e.activation(out=ltmp[:, 2:4], in_=ltmp[:, 0:2], func=AF.Sigmoid),
                 reads=[t_ltmp], writes=[t_ltmp])
        P.op("dve", lambda e: e.memset(lbp[:], 0.0), writes=[t_lbp])
        for pr in range(2):
            P.op("dve", lambda e: e.memset(lbp[:, 0, pr, 0:1], 1.0), reads=[t_lbp], writes=[t_lbp])
            if nl == 2:
                P.op("dve", lambda e: e.tensor_copy(out=lbp[:, 1, pr, 1:2], in_=ltmp[:, 2 + pr:3 + pr]),
                     reads=[t_ltmp, t_lbp], writes=[t_lbp])
                P.op("dve", lambda e: e.tensor_scalar(out=lbp[:, 1, pr, 0:1], in0=ltmp[:, 2 + pr:3 + pr],
                                                      scalar1=-1.0, scalar2=1.0, op0=ALU.mult, op1=ALU.add),
                     reads=[t_ltmp, t_lbp], writes=[t_lbp])

        t_sc = {}
        for l in range(nl):
            for nm in ("fm", "tm", "wo", "gu", "wd"):
                t_sc[(nm, l)] = T("sc_%s%d" % (nm, l))

        def cast(dst, src, t):
            P.dma("pool", dst, src, writes=[t], add=True)

        def emit_casts():
            for l in range(nl):
                for dc in range(8):
                    rows = slice(dc * 128, (dc + 1) * 128)
                    cast(winfm_s[l, :, dc, :], win_d[l, rows, NTM:NTM + NFM], t_sc[("fm", l)])
                for dc in range(8):
                    rows = slice(dc * 128, (dc + 1) * 128)
                    cast(wintm_s[l, :, dc, :], win_d[l, rows, 0:NTM], t_sc[("tm", l)])
                for dc in range(8):
                    rows = slice(dc * 128, (dc + 1) * 128)
                    cast(wout_s[l, :, :, dc, :].rearrange("h p c -> p h c"),
                         wout_d[l, rows, :].rearrange("p (h c) -> p h c", c=512), t_sc[("wo", l)])
                for dc in range(8):
                    rows = slice(dc * 128, (dc + 1) * 128)
                    cast(wg_s[l, :, :, dc, :].rearrange("g p c -> p g c"),
                         wg_d[l, rows, :].rearrange("p (g c) -> p g c", c=256), t_sc[("gu", l)])
                    cast(wu_s[l, :, :, dc, :].rearrange("g p c -> p g c"),
                         wu_d[l, rows, :].rearrange("p (g c) -> p g c", c=256), t_sc[("gu", l)])
                for pc in range(6):
                    ncc = 4 if pc < 5 else 2
                    cast(wd_s[l, pc, :, 0:ncc, :],
                         wd_d[l, pc * 512:pc * 512 + ncc * 128, :].rearrange("(cc p) d -> p cc d", p=128),
                         t_sc[("wd", l)])

        xres = sb("xres", [128, NB, D]); t_x = [T("x%d" % b) for b in range(NB)]
        hT = sb("hT", [128, 8, 2 + TT], BF16); t_hT = T("hT")
        halo = sb("halo", [128, nl, 8, 2], BF16); t_halo = [T("halo%d" % l) for l in range(nl)]
        wfm = sb("wfm", [128, 8, NFM], BF16); t_wfm = T("wfm")
        wtm = sb("wtm", [128, 8, NTM], BF16); t_wtm = T("wtm")
        ring = sb("ring", [128, NRING, 4096], BF16); t_ring = [T("ring%d" % i) for i in range(NRING)]
        aT = sb("aT", [128, NCH, TT], BF16); t_aT = T("aT")
        xo = aT[:].rearrange("p c t -> p (c t)")[:, 0:2 * NB * D].bitcast(F32).rearrange("p (b d) -> p b d", b=NB)
        qT = sb("qT", [128, 4, TT], BF16); t_qT = T("qT")
        kT2 = sb("kT2", [128, nl, 2, 128 + TT], BF16); t_kT2 = [T("kT2%d" % l) for l in range(nl)]
        vaug = sb("vaug", [128, nl, NB + 1, 2, 65], BF16); t_vaug = [T("vaug%d" % l) for l in range(nl)]
        Sst = sb("Sst", [128, nl, 2, HN + 1, 64]); t_S = [T("S%d" % l) for l in range(nl)]
        sqb = sb("sqb", [128, 2, TT]); t_sqb = [T("sqb0"), T("sqb1")]
        epsc = sb("epsc", [128, 1]); t_epsc = T("epsc")
        nstat = sb("nstat", [128, 8]); t_nstat = T("nstat")
        hs = sb("hs", [128, 2, D], BF16); t_hs = [T("hs0"), T("hs1")]
        u4 = sb("u4", [128, NB, 256], BF16); t_u4 = T("u4")
        vv4 = sb("vv4", [128, NB, 256], BF16); t_vv4 = T("vv4")
        vh4 = sb("vh4", [128, NB, 256], BF16); t_vh4 = T("vh4")
        sg4 = sb("sg4", [128, NB, 256], BF16); t_sg4 = T("sg4")
        vtmp = sb("vtmp", [128, 2, 256]); t_vtmp = [T("vtmp0"), T("vtmp1")]
        vsq = sb("vsq", [128, 2, 256]); t_vsq = [T("vsq0"), T("vsq1")]
        vstat = sb("vstat", [128, 2, NB * 4]); t_vstat = T("vstat")
        rst2v = [vtmp[:].rearrange("p a c -> p (a c)")[0:2, :], vsq[:].rearrange("p a c -> p (a c)")[0:2, :]]
        t_rst2 = [t_vtmp, t_vsq]
        vn = sb("vn", [128, 256], BF16); t_vn = T("vn")
        vexp = sb("vexp", [128, 4, HN, 64], BF16); t_vexp = T("vexp")
        mixed2 = sb("mixed", [128, 2, D], BF16); t_mixed2 = [T("mixed0"), T("mixed1")]
        mixT = sb("mixT", [128, 8, 128], BF16); t_mixT = T("mixT")
        junk = mixT[:].rearrange("p c t -> p (c t)"); t_junk = t_mixT
        PT = sb("PT", [128, 2, 4, 128], BF16); t_PT = [T("PT0"), T("PT1")]
        hg = {}
        for nm in ("A", "B", "C", "E"):
            hg[nm] = (sb("hg_" + nm, [128, 2, 128]), T("hg_" + nm))
        for nm in ("qd", "kd", "kl"):
            hg[nm] = (sb("hg_" + nm, [128, 2, 2, 128], BF16), [T("hg_" + nm + "0"), T("hg_" + nm + "1")])
        dec2 = sb("dec", [128, 2, 2, HN]); t_dec2 = [T("dec0"), T("dec1")]
        kltok = sb("kltok", [128, 2, 2, 128], BF16); t_kltok = T("kltok")
        attnT = sb("attnT", [128, 4, 128], BF16); t_attnT = T("attnT")
        Qm = sb("Qm", [128, 2, HN, 128], BF16); t_Qm = T("Qm")
        Sbf = sb("Sbf", [128, 2, HN, 64], BF16); t_Sbf = T("Sbf")
        osq_t = vsq; t_osq = t_vsq[1]
        yc = sb("yc", [128, 256]); t_yc = T("yc")
        ostat = sb("ostat", [128, 8]); t_ostat = T("ostat")
        den = sb("den", [128, 2, 8]); t_den = [T("den0"), T("den1")]
        f1 = sb("f1", [128, 2, 256]); t_f1 = [T("f1a"), T("f1b")]
        f2 = sb("f2", [128, 2, 256]); t_f2 = [T("f2a"), T("f2b")]
        pbs = [st.enter_context(nc.psum_tensor("pb%d" % i, [128, 512], F32)) for i in range(8)]
        t_pb = [T("pb%d" % i) for i in range(8)]
        bank_i = [0]

        def bank():
            i = bank_i[0] % 8
            bank_i[0] += 1
            return pbs[i], t_pb[i]

        P.op("pool", lambda e: e.memset(epsc[:], EPS), writes=[t_epsc])
        P.op("pool", lambda e: e.memset(Qm[:], 0.0), writes=[t_Qm])
        P.op("pool", lambda e: e.memset(kltok[:], 0.0), writes=[t_kltok])
        P.op("pool", lambda e: e.memset(Sst[:], 0.0), writes=t_S)
        P.op("pool", lambda e: e.memset(halo[:], 0.0), writes=t_halo)
        P.op("pool", lambda e: e.memset(kT2[:], 0.0), writes=t_kT2)
        P.op("pool", lambda e: e.memset(vaug[:], 0.0), writes=t_vaug)
        for l in range(nl):
            P.op("pool", lambda e: e.memset(vaug[:, l, :, :, 64:65], 1.0), reads=[t_vaug[l]], writes=[t_vaug[l]])

        emit_casts()

        items = []
        for ti in range(ntile):
            for l in range(nl):
                for hf in range(2):
                    items.append([(wout_s[l, hf].rearrange("p a b -> p (a b)"), 0, 4096, t_sc[("wo", l)])])
                for g in range(11):
                    items.append([(wg_s[l, g].rearrange("p a b -> p (a b)"), 0, 2048, t_sc[("gu", l)]),
                                  (wu_s[l, g].rearrange("p a b -> p (a b)"), 2048, 2048, t_sc[("gu", l)])])
                for pc in range(6):
                    nv = 4096 if pc < 5 else 2048
                    items.append([(wd_s[l, pc].rearrange("p a b -> p (a b)")[:, 0:nv], 0, nv, t_sc[("wd", l)])])
        rstate = {"issued": 0, "consumed": 0, "released": 0}

        def ring_fill():
            while rstate["issued"] < len(items) and rstate["issued"] < rstate["released"] + NRING:
                j = rstate["issued"]
                s = j % NRING
                for k, (src, off, n, tsrc) in enumerate(items[j]):
                    P.dma("sp", ring[:, s, off:off + n], src, reads=[tsrc], writes=[t_ring[s]], add=(k > 0))
                rstate["issued"] += 1

        def ring_done(k=1):
            rstate["released"] += k
            ring_fill()

        def ring_next():
            j = rstate["consumed"]
            assert j < rstate["issued"]
            rstate["consumed"] += 1
            s = j % NRING
            return ring[:, s, :], t_ring[s]

        cp = lambda l, a, b: colpar[:, l, a:b]

        dbgs = {}

        def dbg(name, ap, t, shape):
            if not debug or name in dbgs:
                return
            d = nc.dram_tensor("dbg_" + name, list(shape), ap.dtype, kind="ExternalOutput").ap()
            P.dma("sp", d, ap, reads=[t], writes=[T("dbgo_" + name)])
            dbgs[name] = True

        def ck(k):
            P.mark("ck%s" % k)
            if stop is not None and abs(stop - k) < 1e-6:
                raise _Stop()

        def rstd_act(out, in_, n, t_in, t_out):
            t_in = t_in if isinstance(t_in, list) else [t_in]
            t_out = t_out if isinstance(t_out, list) else [t_out]
            P.op("act", lambda e: e.activation(out=out, in_=in_, func=AF.Ln, scale=1.0 / n,
                                               bias=epsc[0:out.shape[0], :]),
                 reads=t_in + [t_epsc], writes=t_out)
            P.op("act", lambda e: e.activation(out=out, in_=out, func=AF.Exp, scale=-0.5),
                 reads=t_out, writes=t_out)

        def rmsnorm_to_hT(l, gcol0):
            for b in range(NB):
                P.op("act", lambda e: e.activation(out=junk, in_=xres[:, b, :], func=AF.Square,
                                                   accum_out=nstat[:, b:b + 1]),
                     reads=[t_x[b]], writes=[t_junk, t_nstat])
            rstd_act(nstat[:, 4:8], nstat[:, 0:4], D, t_nstat, t_nstat)
            for b in range(NB):
                s_ = b % 2
                P.op("act", lambda e: e.activation(out=hs[:, s_, :], in_=xres[:, b, :], func=AF.Identity,
                                                   scale=nstat[:, 4 + b:5 + b]),
                     reads=[t_x[b], t_nstat], writes=[t_hs[s_]])
                pb, tpb = bank()
                pbb = pb[:].bitcast(BF16)
                for dc in range(8):
                    P.op("pe", lambda e: e.transpose(out=pbb[:, dc * 128:(dc + 1) * 128],
                                                     in_=hs[:, s_, dc * 128:(dc + 1) * 128], identity=ident[:]),
                         reads=[t_hs[s_], t_ident], writes=[tpb])
                P.op("dve", lambda e: e.tensor_tensor(
                    out=hT[:, :, 2 + b * 128:2 + (b + 1) * 128],
                    in0=pbb.rearrange("p (c t) -> p c t", t=128),
                    in1=cp(l, gcol0, gcol0 + 8).unsqueeze(2).to_broadcast([128, 8, 128]), op=ALU.mult),
                    reads=[tpb, t_colpar], writes=[t_hT])

        hTv = hT[:, :, 2:2 + TT]

        def fm_phase(l):
            def main_mm(ft):
                pb, tpb = bank()
                for dc in range(8):
                    P.op("pe", lambda e: e.matmul(pb[:], lhsT=wfm[:, dc, ft * 128:(ft + 1) * 128],
                                                  rhs=hTv[:, dc, :], start=(dc == 0), stop=(dc == 7)),
                         reads=[t_wfm, t_hT], writes=[tpb])
                return pb, tpb

            def dst_of(ft):
                if ft < 4:
                    return qT[:, ft, :], t_qT, 16
                return kT2[:, l, ft - 4, 128:128 + TT], t_kT2[l], 17

            def stage_a(ft):
                pb, tpb = main_mm(ft)
                dst, tdst, _ = dst_of(ft)
                s_ = ft % 2
                P.op("act", lambda e: e.activation(out=dst, in_=pb[:], func=AF.Copy), reads=[tpb], writes=[tdst])
                P.op("act", lambda e: e.activation(out=sqb[:, s_, :], in_=pb[:], func=AF.Square),
                     reads=[tpb], writes=[t_sqb[s_]])

            def stage_b(ft):
                s_ = ft % 2
                pb2, tpb2 = bank()
                P.op("pe", lambda e: e.matmul(pb2[0:2, :], lhsT=blk2[:], rhs=sqb[:, s_, :], start=True, stop=True),
                     reads=[t_blk, t_sqb[s_]], writes=[tpb2])
                rstd_act(rst2v[s_], pb2[0:2, :], 64, tpb2, t_rst2[s_])

            def stage_c(ft):
                s_ = ft % 2
                dst, tdst, gc = dst_of(ft)
                pb3, tpb3 = bank()
                P.op("pe", lambda e: e.matmul(pb3[:], lhsT=blk2T[:], rhs=rst2v[s_], start=True, stop=True),
                     reads=[t_blk] + t_rst2[s_], writes=[tpb3])
                P.op("dve", lambda e: e.scalar_tensor_tensor(out=dst, in0=pb3[:], scalar=cp(l, gc, gc + 1),
                                                             in1=dst, op0=ALU.mult, op1=ALU.mult),
                     reads=[tpb3, tdst, t_colpar], writes=[tdst])

            for step in range(8):
                if step < 6:
                    stage_a(step)
                if 0 <= step - 1 < 6:
                    stage_b(step - 1)
                if 0 <= step - 2 < 6:
                    stage_c(step - 2)
                if step == 5:
                    for ft in (8, 9):
                        pb, tpb = main_mm(ft)
                        P.op("act", lambda e: e.activation(out=cqf[:, ft - 6, :], in_=pb[:], func=AF.Copy),
                             reads=[tpb], writes=[t_cqf])
            for ft in (6, 7):
                pb, tpb = main_mm(ft)
                P.op("act", lambda e: e.activation(out=cqf[:, ft - 6, :], in_=pb[:], func=AF.Silu),
                     reads=[tpb], writes=[t_cqf])

        def tm_phase(l):
            def grp(b, c0, n):
                pb, tpb = bank()
                cols = slice(b * 128, (b + 1) * 128)
                for dc in range(8):
                    P.op("pe", lambda e: e.matmul(pb[:, 0:n], lhsT=hTv[:, dc, cols], rhs=wtm[:, dc, c0:c0 + n],
                                                  start=(dc == 0), stop=(dc == 7)),
                         reads=[t_hT, t_wtm], writes=[tpb])
                return pb, tpb
            for b in range(NB):
                p2, tp2 = grp(b, 512, 512)
                P.op("act", lambda e: e.activation(out=sg4[:, b, :], in_=p2[:, 256:512], func=AF.Silu),
                     reads=[tp2], writes=[t_sg4])
                P.op("act", lambda e: e.activation(out=vh4[:, b, :], in_=p2[:, 0:256], func=AF.Copy),
                     reads=[tp2], writes=[t_vh4])
            for b in range(NB):
                p3, tp3 = grp(b, 1024, 128)
                P.op("act", lambda e: e.activation(
                    out=vaug[:, l, b + 1, :, 0:64], in_=p3[:, 0:128].rearrange("p (g d) -> p g d", d=64),
                    func=AF.Copy), reads=[tp3], writes=[t_vaug[l]])
            for b in range(NB):
                p1, tp1 = grp(b, 0, 512)
                s_ = b % 2
                P.op("act", lambda e: e.activation(out=u4[:, b, :], in_=p1[:, 0:256], func=AF.Gelu),
                     reads=[tp1], writes=[t_u4])
                P.op("act", lambda e: e.activation(out=vtmp[:, s_, :], in_=p1[:, 256:512], func=AF.Gelu),
                     reads=[tp1], writes=[t_vtmp[s_]])
                P.op("pool", lambda e: e.tensor_tensor(out=vsq[:, s_, :], in0=vtmp[:, s_, :], in1=vtmp[:, s_, :], op=ALU.mult),
                     reads=[t_vtmp[s_]], writes=[t_vsq[s_]])
                P.op("dve", lambda e: e.tensor_reduce(out=vstat[:, 0, b * 4:(b + 1) * 4],
                                                      in_=vsq[:, s_, :].rearrange("p (g d) -> p g d", d=64),
                                                      axis=AX.X, op=ALU.add),
                     reads=[t_vsq[s_]], writes=[t_vstat])
                P.op("pool", lambda e: e.tensor_copy(out=vv4[:, b, :], in_=vtmp[:, s_, :]),
                     reads=[t_vtmp[s_]], writes=[t_vv4])

        def chain_a(l, b):
            mixed = mixed2[:, b % 2, :]; t_mixed = t_mixed2[b % 2]
            for g in range(4):
                P.op("dve", lambda e: e.scalar_tensor_tensor(
                    out=vn[:, g * 64:(g + 1) * 64], in0=vv4[:, b, g * 64:(g + 1) * 64],
                    scalar=vstat[:, 1, b * 4 + g:b * 4 + g + 1], in1=rowpar[:, l, g * 64:(g + 1) * 64],
                    op0=ALU.mult, op1=ALU.mult),
                    reads=[t_vv4, t_vstat, t_rowpar], writes=[t_vn])
            yield
            pa, tpa = bank()
            for g in range(4):
                P.op("pe", lambda e: e.matmul(pa[:, g * 64:(g + 1) * 64], lhsT=wsT[:, l, g, :],
                                              rhs=vn[:, g * 64:(g + 1) * 64], start=True, stop=True),
                     reads=[t_wsT, t_vn], writes=[tpa])
            yield
            for g in range(4):
                P.op("dve", lambda e: e.scalar_tensor_tensor(
                    out=mixed[:, g * 64:(g + 1) * 64], in0=pa[:, g * 64:(g + 1) * 64],
                    scalar=bT[:, l, g:g + 1], in1=u4[:, b, g * 64:(g + 1) * 64], op0=ALU.add, op1=ALU.mult),
                    reads=[tpa, t_bT, t_u4], writes=[t_mixed])
                if g % 2 == 1:
                    yield

        def chain_b(l, b, gb):
            mixed = mixed2[:, b % 2, :]; t_mixed = t_mixed2[b % 2]
            cols = slice(b * 128, (b + 1) * 128)
            kbs = [1] if gb == 0 else [0, 1]
            for g in range(2):
                for e2 in range(2):
                    ps_, tps = bank()
                    P.op("pe", lambda e: e.matmul(ps_[:], lhsT=ident[:],
                                                  rhs=bias8[:, g, e2, :, :, :].rearrange("p k r i -> p (k r i)"),
                                                  start=True, stop=False),
                         reads=[t_ident, t_bias8], writes=[tps])
                    pr_ = slice(64 * e2, 64 * e2 + 64)
                    nmm = len(kbs) * 2
                    imm = 0
                    for kb in kbs:
                        for rp in range(2):
                            h = 4 * g + 2 * rp + e2
                            m = h // 2
                            kc = slice(b * 128 + kb * 128, b * 128 + kb * 128 + 128)
                            imm += 1
                            P.op("pe", lambda e: e.matmul(ps_[:, (kb * 2 + rp) * 128:(kb * 2 + rp + 1) * 128],
                                                          lhsT=kT2[pr_, l, g, kc], rhs=qT[pr_, m, cols],
                                                          start=False, stop=(imm == nmm)),
                                 reads=[t_kT2[l], t_qT], writes=[tps])
                    yield
                    P.op("act", lambda e: e.activation(out=PT[:, :, e2:4:2, :],
                                                       in_=ps_[:].rearrange("p (k r i) -> p k r i", k=2, r=2),
                                                       func=AF.Exp, scale=0.125),
                         reads=[tps], writes=[t_PT[0], t_PT[1]])
                    yield
                po, tpo = bank()
                for r in range(4):
                    for kb in kbs:
                        P.op("pe", lambda e: e.matmul(po[:, r * 65:(r + 1) * 65], lhsT=PT[:, kb, r, :],
                                                      rhs=vaug[:, l, b + kb, g, :], start=(kb == kbs[0]),
                                                      stop=(kb == 1)),
                             reads=[t_PT[kb], t_vaug[l]], writes=[tpo])
                yield
                pov = po[:, 0:260].rearrange("p (r d) -> p r d", d=65)
                P.op("dve", lambda e: e.tensor_tensor(out=den[:, g, 0:4].unsqueeze(2), in0=pov[:, :, 64:65],
                                                      in1=esink[:, l, 4 * g:4 * g + 4].unsqueeze(2), op=ALU.add),
                     reads=[tpo, t_esink], writes=[t_den[g]])
                yield
                P.op("dve", lambda e: e.reciprocal(out=den[:, g, 4:8], in_=den[:, g, 0:4]),
                     reads=[t_den[g]], writes=[t_den[g]])
                yield
                P.op("dve", lambda e: e.tensor_tensor(
                    out=mixed[:, 256 + g * 256:256 + (g + 1) * 256].rearrange("p (r d) -> p r d", d=64),
                    in0=pov[:, :, 0:64], in1=den[:, g, 4:8].unsqueeze(2).to_broadcast([128, 4, 64]), op=ALU.mult),
                    reads=[tpo, t_den[g]], writes=[t_mixed])
                yield

        def chain_c1(l, b):
            cols = slice(b * 128, (b + 1) * 128)
            pp = b % 2
            A, tA = hg["A"]; Bq, tB = hg["B"]; C, tC = hg["C"]; E, tE = hg["E"]
            qd, tqd = hg["qd"][0][:, pp], hg["qd"][1][pp]
            kd, tkd = hg["kd"][0][:, pp], hg["kd"][1][pp]
            kl, tkl = hg["kl"][0][:, pp], hg["kl"][1][pp]
            dec, t_dec = dec2[:, pp], t_dec2[pp]
            q_ = cqf[:, 0:2, cols]
            zf = cqf[:, 2:4, cols]
            P.op("act", lambda e: e.activation(out=A[:], in_=zf, func=AF.Exp, scale=-1.0), reads=[t_cqf], writes=[tA])
            yield
            P.op("act", lambda e: e.activation(out=A[:], in_=A[:], func=AF.Ln, bias=1.0), reads=[tA], writes=[tA])
            yield
            P.op("act", lambda e: e.activation(out=A[:], in_=A[:], func=AF.Exp, scale=-1.0), reads=[tA], writes=[tA])
            yield
            for pr in range(2):
                P.op("dve", lambda e: e.tensor_scalar(out=Bq[:, pr, :], in0=A[:, pr, :],
                                                      scalar1=lbp[:, l, pr, 0:1], scalar2=lbp[:, l, pr, 1:2],
                                                      op0=ALU.mult, op1=ALU.add),
                     reads=[tA, t_lbp], writes=[tB])
            yield
            P.op("act", lambda e: e.activation(out=A[:], in_=Bq[:], func=AF.Ln), reads=[tB], writes=[tA])
            yield
            for pr in range(2):
                P.op("dve", lambda e: e.tensor_tensor_scan(out=C[:, pr, :], data0=cst[:, 2, :],
                                                           data1=A[:, pr, :], initial=0.0,
                                                           op0=ALU.mult, op1=ALU.add),
                     reads=[tA, t_cst], writes=[tC])
            yield
            P.op("pool", lambda e: e.tensor_scalar(out=Bq[:], in0=Bq[:], scalar1=-1.0, scalar2=1.0,
                                                   op0=ALU.mult, op1=ALU.add), reads=[tB], writes=[tB])
            yield
            P.op("act", lambda e: e.activation(out=A[:], in_=C[:], func=AF.Exp), reads=[tC], writes=[tA])
            P.op("act", lambda e: e.activation(out=E[:], in_=C[:], func=AF.Exp, scale=-1.0), reads=[tC], writes=[tE])
            cum4 = C[:].rearrange("p a (n j) -> p (a n) j", j=HCH)
            P.op("act", lambda e: e.activation(out=dec.rearrange("p a n -> p (a n)").unsqueeze(2),
                                               in_=cum4[:, :, HCH - 1:HCH], func=AF.Exp),
                 reads=[tC], writes=[t_dec])
            yield
            P.op("dve", lambda e: e.tensor_tensor(out=qd, in0=q_, in1=A[:], op=ALU.mult),
                 reads=[t_cqf, tA], writes=[tqd])
            yield
            P.op("pool", lambda e: e.tensor_tensor(out=kd, in0=Bq[:], in1=E[:], op=ALU.mult),
                 reads=[tB, tE], writes=[tkd])
            yield
            P.op("dve", lambda e: e.tensor_tensor(
                out=A[:].rearrange("p a (n j) -> p (a n) j", j=HCH),
                in0=cum4[:, :, HCH - 1:HCH].to_broadcast([128, 2 * HN, HCH]), in1=cum4, op=ALU.subtract),
                reads=[tC, tA], writes=[tA])
            yield
            P.op("act", lambda e: e.activation(out=A[:], in_=A[:], func=AF.Exp), reads=[tA], writes=[tA])
            yield
            P.op("pool", lambda e: e.tensor_tensor(out=kl, in0=Bq[:], in1=A[:], op=ALU.mult),
                 reads=[tB, tA], writes=[tkl])
            yield

        def emit_vexp(b):
            for h in range(4):
                P.op("pool", lambda e: e.tensor_tensor(
                    out=vexp[:, h, :, :], in0=vh4[:, b, h * 64:(h + 1) * 64].unsqueeze(1).to_broadcast([128, HN, 64]),
                    in1=cm[:].unsqueeze(2).to_broadcast([128, HN, 64]), op=ALU.mult),
                    reads=[t_vh4, t_cm], writes=[t_vexp])

        def chain_c2(l, b):
            pp = b % 2
            mixed = mixed2[:, pp, :]; t_mixed = t_mixed2[pp]
            qd, tqd = hg["qd"][0][:, pp], hg["qd"][1][pp]
            kd, tkd = hg["kd"][0][:, pp], hg["kd"][1][pp]
            kl, tkl = hg["kl"][0][:, pp], hg["kl"][1][pp]
            dec, t_dec = dec2[:, pp], t_dec2[pp]
            if b == 0:
                emit_vexp(0)
                yield
            for pr in range(2):
                v = Qm[:, pr, :, :]
                dst = bass.AP(v.tensor, v.offset, [list(v.ap[0]), [128 + HCH, HN], [1, HCH]])
                P.op("pool", lambda e: e.tensor_copy(out=dst, in_=qd[:, pr, :].rearrange("p (n j) -> p n j", j=HCH)),
                     reads=[tqd], writes=[t_Qm])
            for e2 in range(2):
                pat, tpat = bank()
                rows = slice(64 * e2, 64 * e2 + 64)
                for pr in range(2):
                    P.op("pe", lambda e: e.matmul(pat[:, pr * 128:(pr + 1) * 128], lhsT=kd[rows, pr, :],
                                                  rhs=qd[rows, pr, :], start=True, stop=True),
                         reads=[tkd, tqd], writes=[tpat])
                yield
                P.op("dve", lambda e: e.tensor_tensor(
                    out=attnT[:, e2:4:2, :], in0=pat[:, 0:256].rearrange("p (h t) -> p h t", t=128),
                    in1=cst[:, 1, :].unsqueeze(1).to_broadcast([128, 2, 128]), op=ALU.mult),
                    reads=[tpat, t_cst], writes=[t_attnT])
                yield
            pk, tpk = bank()
            pkb = pk[:].bitcast(BF16)
            for pr in range(2):
                P.op("pe", lambda e: e.transpose(out=pkb[:, pr * 128:(pr + 1) * 128], in_=kl[:, pr, :],
                                                 identity=ident[:]),
                     reads=[tkl, t_ident], writes=[tpk])
            yield
            for pr in range(2):
                v = kltok[:, pr, :, :]
                dstk = bass.AP(v.tensor, v.offset, [list(v.ap[0]), [192, 2], [1, 64]])
                P.op("act", lambda e: e.activation(out=dstk, in_=pkb[:, pr * 128:(pr + 1) * 128].rearrange("p (a k) -> p a k", k=64),
                                                   func=AF.Copy),
                     reads=[tpk], writes=[t_kltok])
            yield
            pds = []
            for pr in range(2):
                pd_, tpd = bank()
                for e2 in range(2):
                    h = 2 * pr + e2
                    P.op("pe", lambda e: e.matmul(pd_[:, 0:HN * 64], lhsT=kltok[:, pr, e2, :],
                                                  rhs=vexp[:, h, :, :].rearrange("p n v -> p (n v)"),
                                                  start=(e2 == 0), stop=(e2 == 1)),
                         reads=[t_kltok, t_vexp], writes=[tpd])
                pds.append((pd_, tpd))
                yield
            if b + 1 < NB:
                emit_vexp(b + 1)
            for n in range(HN):
                for pr in range(2):
                    pd_, tpd = pds[pr]
                    P.op("dve", lambda e: e.scalar_tensor_tensor(
                        out=Sst[:, l, pr, n + 1, :], in0=Sst[:, l, pr, n, :], scalar=dec[:, pr, n:n + 1],
                        in1=pd_[:, n * 64:(n + 1) * 64], op0=ALU.mult, op1=ALU.add),
                        reads=[t_S[l], t_dec, tpd], writes=[t_S[l]])
                yield
            P.op("act", lambda e: e.activation(out=Sbf[:], in_=Sst[:, l, :, 0:HN, :], func=AF.Copy),
                 reads=[t_S[l]], writes=[t_Sbf])
            P.op("pool", lambda e: e.tensor_copy(out=Sst[:, l, :, 0, :], in_=Sst[:, l, :, HN, :]),
                 reads=[t_S[l]], writes=[t_S[l]])
            yield
            pqs = []
            for e2 in range(2):
                pq, tpq = bank()
                rows = slice(64 * e2, 64 * e2 + 64)
                for pr in range(2):
                    h = 2 * pr + e2
                    P.op("pe", lambda e: e.matmul(pq[:, pr * 64:(pr + 1) * 64], lhsT=attnT[:, h, :],
                                                  rhs=vh4[:, b, h * 64:(h + 1) * 64], start=True, stop=False),
                         reads=[t_attnT, t_vh4], writes=[tpq])
                    for n in range(HN):
                        P.op("pe", lambda e: e.matmul(pq[:, pr * 64:(pr + 1) * 64], lhsT=Qm[rows, pr, n, :],
                                                      rhs=Sbf[rows, pr, n, :], start=False, stop=(n == HN - 1)),
                             reads=[t_Qm, t_Sbf], writes=[tpq])
                    yield
                pqs.append((pq, tpq))

            def hv(t, e2):
                return t.rearrange("p (a e d) -> p a e d", e=2, d=64)[:, :, e2, :]
            for e2 in range(2):
                pq, tpq = pqs[e2]
                pq3 = pq[:, 0:128].rearrange("p (a d) -> p a d", d=64)
                P.op("act", lambda e: e.activation(out=hv(vsq[:, 1, :], e2), in_=pq3, func=AF.Square),
                     reads=[tpq], writes=[t_osq])
                P.op("dve", lambda e: e.tensor_reduce(out=ostat[:, 2 * e2:2 * e2 + 2], in_=hv(vsq[:, 1, :], e2), axis=AX.X, op=ALU.add),
                     reads=[t_osq], writes=[t_ostat])
                yield
            rstd_act(ostat[:, 4:8], ostat[:, 0:4], 64, t_ostat, t_ostat)
            yield
            for e2 in range(2):
                pq, tpq = pqs[e2]
                for pr in range(2):
                    h = 2 * pr + e2
                    P.op("dve", lambda e: e.scalar_tensor_tensor(
                        out=mixed[:, 768 + h * 64:768 + (h + 1) * 64], in0=pq[:, pr * 64:(pr + 1) * 64],
                        scalar=ostat[:, 4 + 2 * e2 + pr:5 + 2 * e2 + pr], in1=sg4[:, b, h * 64:(h + 1) * 64],
                        op0=ALU.mult, op1=ALU.mult),
                        reads=[tpq, t_ostat, t_sg4], writes=[t_mixed])
                yield

        def interleave(gens):
            ent = []
            for g in gens:
                if isinstance(g, tuple):
                    ent.append([g[0], g[1], g[2]])
                else:
                    ent.append([g, 1, 0])
            rnd = 0
            while ent:
                for e_ in list(ent):
                    if e_[2] > rnd:
                        continue
                    for _ in range(e_[1]):
                        try:
                            next(e_[0])
                        except StopIteration:
                            ent.remove(e_)
                            break
                rnd += 1

        def out_proj(b, wo):
            mixed = mixed2[:, b % 2, :]; t_mixed = t_mixed2[b % 2]
            pm_, tpm = bank()
            yield
            pmb = pm_[:].bitcast(BF16)
            for dc in range(8):
                P.op("pe", lambda e: e.transpose(out=pmb[:, dc * 128:(dc + 1) * 128],
                                                 in_=mixed[:, dc * 128:(dc + 1) * 128], identity=ident[:]),
                     reads=[t_mixed, t_ident], writes=[tpm])
            yield
            P.op("act", lambda e: e.activation(out=mixT[:].rearrange("p c t -> p (c t)"), in_=pmb, func=AF.Copy),
                 reads=[tpm], writes=[t_mixT])
            yield
            for hf in range(2):
                po2, tpo2 = bank()
                wsl, twsl = wo[hf]
                for dc in range(8):
                    P.op("pe", lambda e: e.matmul(po2[:], lhsT=mixT[:, dc, :], rhs=wsl[:, dc * 512:(dc + 1) * 512],
                                                  start=(dc == 0), stop=(dc == 7)),
                         reads=[t_mixT, twsl], writes=[tpo2])
                P.op("dve", lambda e: e.tensor_tensor(out=xres[:, b, hf * 512:(hf + 1) * 512], in0=po2[:],
                                                      in1=xres[:, b, hf * 512:(hf + 1) * 512], op=ALU.add),
                     reads=[tpo2, t_x[b]], writes=[t_x[b]])
                yield

        def ffn(l):
            units = [(g, cc, sub) for g in range(11) for cc in range(2) for sub in range(2)]
            pend = None
            gu = tgu = None

            def stage2(p_):
                c, sub, s_, pup, tpup = p_
                P.op("act", lambda e: e.activation(out=f2[:, s_, :], in_=f1[:, s_, :], func=AF.Silu),
                     reads=[t_f1[s_]], writes=[t_f2[s_]])
                P.op("dve", lambda e: e.tensor_tensor(out=aT[:, c, sub * 256:(sub + 1) * 256], in0=f2[:, s_, :],
                                                      in1=pup[:, 0:256], op=ALU.mult),
                     reads=[t_f2[s_], tpup], writes=[t_aT])

            for k, (g, cc, sub) in enumerate(units):
                if cc == 0 and sub == 0:
                    if g > 0:
                        ring_done(1)
                    ring_fill()
                    gu, tgu = ring_next()
                c = 2 * g + cc
                cw = 22 + 4 * c
                pgt, tpgt = bank()
                pup, tpup = bank()
                for dc in range(8):
                    P.op("pe", lambda e: e.matmul(pgt[:, 0:258], lhsT=gu[:, dc * 256 + cc * 128: dc * 256 + (cc + 1) * 128],
                                                  rhs=hT[:, dc, sub * 256: sub * 256 + 258],
                                                  start=(dc == 0), stop=(dc == 7)),
                         reads=[tgu, t_hT], writes=[tpgt])
                for dc in range(8):
                    P.op("pe", lambda e: e.matmul(pup[:, 0:256], lhsT=gu[:, 2048 + dc * 256 + cc * 128: 2048 + dc * 256 + (cc + 1) * 128],
                                                  rhs=hT[:, dc, 2 + sub * 256: 2 + sub * 256 + 256],
                                                  start=(dc == 0), stop=(dc == 7)),
                         reads=[tgu, t_hT], writes=[tpup])
                s_ = k % 2
                P.op("act", lambda e: e.activation(out=f1[:, s_, :], in_=pgt[:, 2:258], func=AF.Identity,
                                                   scale=cp(l, cw + 2, cw + 3), bias=cp(l, cw + 3, cw + 4)),
                     reads=[tpgt, t_colpar], writes=[t_f1[s_]])
                P.op("dve", lambda e: e.scalar_tensor_tensor(out=f2[:, s_, :], in0=pgt[:, 1:257],
                                                             scalar=cp(l, cw + 1, cw + 2), in1=f1[:, s_, :],
                                                             op0=ALU.mult, op1=ALU.add),
                     reads=[tpgt, t_colpar, t_f1[s_]], writes=[t_f2[s_]])
                P.op("dve", lambda e: e.scalar_tensor_tensor(out=f1[:, s_, :], in0=pgt[:, 0:256],
                                                             scalar=cp(l, cw, cw + 1), in1=f2[:, s_, :],
                                                             op0=ALU.mult, op1=ALU.add),
                     reads=[tpgt, t_colpar, t_f2[s_]], writes=[t_f1[s_]])
                if pend is not None:
                    stage2(pend)
                pend = (c, sub, s_, pup, tpup)
            stage2(pend)
            ring_done(1)
            ck(9)
            dbank = [[bank() for hf in range(2)] for b in range(NB)]
            for pc in range(6):
                ring_fill()
                wdp, twdp = ring_next()
                for cc in range(4):
                    c = pc * 4 + cc
                    if c >= NCH:
                        break
                    for b in range(NB):
                        for hf in range(2):
                            pbk, tpbk = dbank[b][hf]
                            P.op("pe", lambda e: e.matmul(pbk[:], lhsT=aT[:, c, b * 128:(b + 1) * 128],
                                                          rhs=wdp[:, cc * 1024 + hf * 512: cc * 1024 + (hf + 1) * 512],
                                                          start=(c == 0), stop=(c == NCH - 1)),
                                 reads=[t_aT, twdp], writes=[tpbk])
                ring_done(1)
            for b in range(NB):
                for hf in range(2):
                    pbk, tpbk = dbank[b][hf]
                    if l == nl - 1:
                        P.op("dve", lambda e: e.tensor_tensor(out=xo[:, b, hf * 512:(hf + 1) * 512], in0=pbk[:],
                                                              in1=xres[:, b, hf * 512:(hf + 1) * 512], op=ALU.add),
                             reads=[tpbk, t_x[b]], writes=[t_aT])
                    else:
                        P.op("dve", lambda e: e.tensor_tensor(out=xres[:, b, hf * 512:(hf + 1) * 512], in0=pbk[:],
                                                              in1=xres[:, b, hf * 512:(hf + 1) * 512], op=ALU.add),
                             reads=[tpbk, t_x[b]], writes=[t_x[b]])

        def main_loop():
            for b in range(NB):
                P.dma("sp", xres[:, b, :], x_d[b * 128:(b + 1) * 128, :], writes=[t_x[b]])
            P.dma("sp", wfm[:], winfm_s[0], reads=[t_sc[("fm", 0)]], writes=[t_wfm])
            P.dma("sp", wtm[:], wintm_s[0], reads=[t_sc[("tm", 0)]], writes=[t_wtm])
            ring_fill()
            for ti in range(ntile):
                for b in range(NB if ti > 0 else 0):
                    P.dma("pool", xres[:, b, :],
                          x_d[ti * TT + b * 128: ti * TT + (b + 1) * 128, :], writes=[t_x[b]])
                for l in range(nl):
                    last = (ti == ntile - 1 and l == nl - 1)
                    rmsnorm_to_hT(l, 0)
                    ck(2)
                    fm_phase(l)
                    if not last:
                        P.dma("sp", wfm[:], winfm_s[(l + 1) % nl], reads=[t_sc[("fm", (l + 1) % nl)]], writes=[t_wfm])
                    ck(3)
                    tm_phase(l)
                    if not last:
                        P.dma("sp", wtm[:], wintm_s[(l + 1) % nl], reads=[t_sc[("tm", (l + 1) % nl)]], writes=[t_wtm])
                    ck(4)
                    P.op("pool", lambda e: e.tensor_tensor(
                        out=sg4[:], in0=sg4[:], in1=rowpar[:, l, 256:512].unsqueeze(1).to_broadcast([128, NB, 256]),
                        op=ALU.mult), reads=[t_sg4, t_rowpar], writes=[t_sg4])
                    rstd_act(vstat[:, 1, :], vstat[:, 0, :], 64, t_vstat, t_vstat)
                    ring_fill()
                    wo = [ring_next(), ring_next()]
                    interleave([chain_c1(l, 0)])
                    for b in range(NB):
                        gb = ti * NB + b
                        gens = [(chain_c2(l, b), 2, 0), chain_b(l, b, gb), (chain_a(l, b), 1, 3)]
                        if b + 1 < NB:
                            gens.append((chain_c1(l, b + 1), 1, 2))
                        if b > 0:
                            gens.append((out_proj(b - 1, wo), 1, 1))
                        interleave(gens)
                        ck(6)
                    interleave([out_proj(NB - 1, wo)])
                    ck(8)
                    ring_done(2)
                    P.op("pool", lambda e: e.tensor_copy(out=kT2[:, l, :, 0:128], in_=kT2[:, l, :, TT:TT + 128]),
                         reads=[t_kT2[l]], writes=[t_kT2[l]])
                    P.op("pool", lambda e: e.tensor_copy(out=vaug[:, l, 0, :, :], in_=vaug[:, l, NB, :, :]),
                         reads=[t_vaug[l]], writes=[t_vaug[l]])
                    rmsnorm_to_hT(l, 8)
                    P.op("pool", lambda e: e.tensor_copy(out=hT[:, :, 0:2], in_=halo[:, l, :, :]),
                         reads=[t_halo[l], t_hT], writes=[t_hT])
                    P.op("pool", lambda e: e.tensor_copy(out=halo[:, l, :, :], in_=hT[:, :, TT:TT + 2]),
                         reads=[t_hT, t_halo[l]], writes=[t_halo[l]])
                    ffn(l)
                t_o = T("out")
                for b in range(NB):
                    P.dma("pool", out_d[ti * TT + b * 128: ti * TT + (b + 1) * 128, :], xo[:, b, :], reads=[t_aT],
                          writes=[t_o], add=True)

        try:
            main_loop()
        except _Stop:
            pass
        P.drain("sp")
    build.stats = (P.nops, P.nwait, dict(P.cnt))
    build.marks = P.marks
    return nc


def _t5_bucket(dist):
    max_exact = 16
    n = np.maximum(dist, 0)
    is_small = n < max_exact
    nf = np.maximum(n, 1).astype(np.float32)
    large = max_exact + (np.log(nf / max_exact) / np.log(128 / max_exact) * (32 - max_exact)).astype(np.int32)
    large = np.minimum(large, 31)
    return np.where(is_small, n, large)


def prep_shared(inp, nl=NL):
    f = np.float32
    w_in = np.asarray(inp["w_in"], f)
    tm = list(range(0, 512)) + list(range(1792, 2304)) + list(range(1152, 1280))
    fm = (list(range(512, 1024)) + list(range(1024, 1088)) * 2 + list(range(1088, 1152)) * 2
          + list(range(1280, 1536)) + list(range(1536, 1792)))
    idx = np.array(tm + fm)
    sh = {}
    sh["w_in_r"] = np.ascontiguousarray(w_in[:nl][:, :, idx])
    sh["w_out"] = np.ascontiguousarray(np.asarray(inp["w_out"], f)[:nl])
    sh["w_gate"] = np.ascontiguousarray(np.asarray(inp["w_gate"], f)[:nl])
    sh["w_up"] = np.ascontiguousarray(np.asarray(inp["w_up"], f)[:nl])
    sh["w_down"] = np.ascontiguousarray(np.asarray(inp["w_down"], f)[:nl])
    colpar = np.zeros((128, nl, 110), f)
    lg = np.asarray(inp["hgrn_lb_logits"], f)
    for l in range(nl):
        colpar[:, l, 0:8] = np.asarray(inp["norm1_g"], f)[l].reshape(8, 128).T
        colpar[:, l, 8:16] = np.asarray(inp["norm2_g"], f)[l].reshape(8, 128).T
        colpar[:, l, 16] = np.tile(np.asarray(inp["q_norm_g"], f)[l], 2)
        colpar[:, l, 17] = np.tile(np.asarray(inp["k_norm_g"], f)[l], 2)
        colpar[:, l, 18:20] = lg[0].reshape(2, 128).T
        colpar[:, l, 20:22] = lg[min(1, lg.shape[0] - 1)].reshape(2, 128).T
        cw = np.asarray(inp["conv_w"], f)[l]
        cb = np.asarray(inp["conv_b"], f)[l]
        blk = np.stack([cw[0], cw[1], cw[2], cb], axis=-1).reshape(NCH, 128, 4)
        colpar[:, l, 22:110] = blk.transpose(1, 0, 2).reshape(128, 88)
    sh["colpar"] = colpar
    rowpar = np.zeros((nl, 520), f)
    for l in range(nl):
        rowpar[l, 0:256] = np.asarray(inp["gmlp_vnorm_g"], f)[l].reshape(256)
        rowpar[l, 256:512] = np.tile(np.asarray(inp["hgrn_onorm_g"], f)[l], 4)
        rowpar[l, 512:520] = np.asarray(inp["attn_sinks"], f)[l]
    sh["rowpar"] = rowpar
    ws = np.asarray(inp["gmlp_w_s"], f)[:nl]
    sh["wsT"] = np.ascontiguousarray(ws.transpose(0, 3, 1, 2))
    sh["bT"] = np.ascontiguousarray(np.asarray(inp["gmlp_b_s"], f)[:nl].transpose(0, 2, 1))
    rb = np.asarray(inp["rel_bias"], f)
    j = np.arange(128)[:, None]
    i = np.arange(128)[None, :]
    biasT = np.full((128, 2, 8, 128), -30000.0, f)
    d_prev = i + 128 - j
    d_cur = i - j
    for kb, dd in ((0, d_prev), (1, d_cur)):
        valid = (dd >= 0) & (dd < 128)
        g = rb[_t5_bucket(dd)]
        g = np.where(valid[:, :, None], g, np.float32(-30000.0))
        biasT[:, kb, :, :] = g.transpose(0, 2, 1)
    sh["biasT"] = biasT
    s_ = np.arange(128)[:, None]
    t_ = np.arange(128)[None, :]
    cst = np.zeros((128, 3, 128), f)
    cst[:, 0, :] = (s_ <= t_)
    cst[:, 1, :] = (s_ <= t_) & ((s_ // HCH) == (t_ // HCH))
    cst[:, 2, :] = np.broadcast_to((np.arange(128) % HCH != 0)[None, :], (128, 128))
    sh["cst"] = cst
    sh["cm"] = (np.arange(128)[:, None] // HCH == np.arange(HN)[None, :]).astype(f)
    b2 = np.zeros((128, 2), f)
    b2[0:64, 0] = 1
    b2[64:128, 1] = 1
    sh["blk2"] = b2
    sh["blk2T"] = np.ascontiguousarray(b2.T)
    return sh


_NC_CACHE = {}


def kernel(**inputs):
    x = np.asarray(inputs["x"], np.float32)
    B, S, _ = x.shape
    sh = prep_shared(inputs)
    key = (S, NL)
    if key not in _NC_CACHE:
        _NC_CACHE[key] = build(S, NL)
    nc = _NC_CACHE[key]
    in_maps = []
    for c in range(8):
        m = dict(sh)
        m["x"] = np.ascontiguousarray(x[c % B])
        in_maps.append(m)
    res = run_bass_kernel_spmd(nc, in_maps, core_ids=list(range(8)))
    out = np.stack([np.asarray(res.results[b]["out"], np.float32) for b in range(B)], axis=0)
    return out
```

```python
import contextlib
import numpy as np
import concourse.bass as bass
import concourse.mybir as mybir
from concourse.bass_utils import run_bass_kernel_spmd

F32 = mybir.dt.float32
BF16 = mybir.dt.bfloat16
AF = mybir.ActivationFunctionType
ALU = mybir.AluOpType
AX = mybir.AxisListType

D = 1024
DFF = 2816
NCH = 22
NL = 2
TT = 512
NB = 4
EPS = 1e-6
SOFT_C = 0.0
NDMASEM = 12
NRING = 4
NTM = 1152
HCH = 32
HN = 128 // HCH
NFM = 1280


class T:
    __slots__ = ("name", "w", "r")

    def __init__(self, name):
        self.name = name
        self.w = {}
        self.r = {}


class Prog:
    def __init__(self, nc, stack):
        self.nc = nc
        self.st = stack
        self.eng = {"pe": nc.tensor, "act": nc.scalar, "dve": nc.vector,
                    "pool": nc.gpsimd, "sp": nc.sync}
        self.sem = {e: stack.enter_context(nc.semaphore("s_" + e)) for e in self.eng}
        self.cnt = {e: 0 for e in self.eng}
        self.dsem = {}
        self.dcnt = {}
        self.nds = {"sp": NDMASEM, "pool": 80, "act": 1}
        for q in ("sp", "pool", "act"):
            self.dsem[q] = [stack.enter_context(nc.semaphore("d_%s%d" % (q, i)))
                            for i in range(self.nds[q])]
            self.dcnt[q] = 0
        self.seen = {e: {} for e in self.eng}
        self.nwait = 0
        self.nops = 0
        self.marks = []

    def mark(self, name):
        self.marks.append((name, dict(self.cnt)))

    def sb(self, name, shape, dt=F32):
        return self.st.enter_context(self.nc.sbuf_tensor("sb_" + name, list(shape), dt))

    def _wait(self, e, tok, raw=False):
        kind, key, val = tok
        if kind == "eng":
            if key == e and not (raw and e != "pe"):
                return
            sem = self.sem[key]
            sk = "e_" + key
        else:
            q, i = key
            sem = self.dsem[q][i]
            sk = "d_%s%d" % (q, i)
        if self.seen[e].get(sk, 0) >= val:
            return
        self.eng[e].wait_ge(sem, val)
        self.seen[e][sk] = val
        self.nwait += 1

    def _deps(self, e, reads, writes, add):
        for t in reads:
            for tok in t.w.values():
                self._wait(e, tok, raw=True)
        for t in writes:
            if not add:
                for tok in t.w.values():
                    self._wait(e, tok, raw=True)
            for tok in t.r.values():
                self._wait(e, tok, raw=True)

    def _mark(self, tok, reads, writes, add):
        for t in reads:
            t.r[tok[1]] = tok
        for t in writes:
            if add:
                t.w[tok[1]] = tok
            else:
                t.w = {tok[1]: tok}
            t.r = {}

    def op(self, e, fn, reads=(), writes=()):
        self._deps(e, reads, writes, False)
        ins = fn(self.eng[e])
        self.cnt[e] += 1
        ins.then_inc(self.sem[e], 1)
        self._mark(("eng", e, self.cnt[e]), reads, writes, False)
        self.nops += 1
        return ins

    def dma(self, q, out, in_, reads=(), writes=(), add=False, **kw):
        n = self.dcnt[q]
        slot = n % self.nds[q]
        gen = n // self.nds[q]
        if gen > 0:
            self._wait(q, ("dma", (q, slot), 16 * gen))
        self._deps(q, reads, writes, add)
        ins = self.eng[q].dma_start(out=out, in_=in_, **kw)
        ins.then_inc(self.dsem[q][slot], 16)
        self.dcnt[q] = n + 1
        self._mark(("dma", (q, slot), 16 * (gen + 1)), reads, writes, add)
        return ins

    def drain(self, e, queues=("sp", "pool", "act")):
        for q in queues:
            n = self.dcnt[q]
            for slot in range(self.nds[q]):
                if n > slot:
                    k = (n - 1 - slot) // self.nds[q] + 1
                    self._wait(e, ("dma", (q, slot), 16 * k))


class _Stop(Exception):
    pass


def build(ntok, nl=NL, stop=None, debug=False):
    assert ntok % TT == 0
    ntile = ntok // TT
    nc = bass.Bass("TRN2", target_bir_lowering=False)

    def din(name, shape, dt=F32):
        return nc.dram_tensor(name, list(shape), dt, kind="ExternalInput").ap()

    x_d = din("x", [ntok, D])
    win_d = din("w_in_r", [nl, D, NTM + NFM])
    wout_d = din("w_out", [nl, D, D])
    wg_d = din("w_gate", [nl, D, DFF])
    wu_d = din("w_up", [nl, D, DFF])
    wd_d = din("w_down", [nl, DFF, D])
    colpar_d = din("colpar", [128, nl, 110])
    rowpar_d = din("rowpar", [nl, 520])
    wsT_d = din("wsT", [nl, 128, 4, 128])
    bT_d = din("bT", [nl, 128, 4])
    biasT_d = din("biasT", [128, 2, 8, 128])
    cst_d = din("cst", [128, 3, 128])
    cm_d = din("cm", [128, HN])
    blk2_d = din("blk2", [128, 2])
    blk2T_d = din("blk2T", [2, 128])
    out_d = nc.dram_tensor("out", [ntok, D], F32, kind="ExternalOutput").ap()

    def dscr(name, shape):
        return nc.dram_tensor(name, list(shape), BF16, kind="Internal").ap()

    winfm_s = dscr("winfm_s", [nl, 128, 8, NFM])
    wintm_s = dscr("wintm_s", [nl, 128, 8, NTM])
    wout_s = dscr("wout_s", [nl, 2, 128, 8, 512])
    wg_s = dscr("wg_s", [nl, 11, 128, 8, 256])
    wu_s = dscr("wu_s", [nl, 11, 128, 8, 256])
    wd_s = dscr("wd_s", [nl, 6, 128, 4, 1024])

    with contextlib.ExitStack() as st:
        P = Prog(nc, st)
        sb = P.sb

        ident = sb("ident", [128, 128], BF16); t_ident = T("ident")
        cst = sb("cst", [128, 3, 128]); t_cst = T("cst")
        cm = sb("cm", [128, HN]); t_cm = T("cm")
        blk2 = sb("blk2", [128, 2]); blk2T = sb("blk2T", [2, 128]); t_blk = T("blk")
        colpar = sb("colpar", [128, nl, 110]); t_colpar = T("colpar")
        rowpar = sb("rowpar", [128, nl, 520]); t_rowpar = T("rowpar")
        esink = sb("esink", [128, nl, 8]); t_esink = T("esink")
        wsT = sb("wsT", [128, nl, 4, 128], BF16); t_wsT = T("wsT")
        bT = sb("bT", [128, nl, 4]); t_bT = T("bT")
        bias8 = sb("bias8", [128, 2, 2, 2, 2, 128], BF16); t_bias8 = T("bias8")
        lbp = sb("lbp", [128, nl, 2, 2]); t_lbp = T("lbp")

        cqf = sb("cqf", [128, 4, TT]); t_cqf = T("cqf")
        stage = cqf[:].rearrange("p a t -> p (a t)").rearrange("p (a b c) -> p a b c", a=2, b=8)
        t_stage = t_cqf
        P.op("pool", lambda e: e.memset(ident[:], 0.0), writes=[t_ident])
        P.op("pool", lambda e: e.affine_select(out=ident[:], in_=ident[:], pattern=[[-1, 128]],
                                               compare_op=ALU.not_equal, fill=1.0, base=0,
                                               channel_multiplier=1),
             reads=[t_ident], writes=[t_ident])
        P.dma("sp", cst[:], cst_d, writes=[t_cst])
        P.dma("sp", cm[:], cm_d, writes=[t_cm])
        P.dma("sp", blk2[:], blk2_d, writes=[t_blk])
        P.dma("sp", blk2T[:], blk2T_d, writes=[t_blk], add=True)
        P.dma("sp", colpar[:], colpar_d, writes=[t_colpar])
        for l in range(nl):
            P.dma("sp", rowpar[:, l, :], rowpar_d[l].partition_broadcast(128),
                  writes=[t_rowpar], add=(l > 0))
        P.dma("sp", bT[:], bT_d.rearrange("l p g -> p l g"), writes=[t_bT])
        for l in range(nl):
            P.dma("sp", stage[:, 0, 0:4, :], wsT_d[l], writes=[t_stage])
            P.op("dve", lambda e: e.tensor_tensor(
                out=wsT[:, l, :, :], in0=stage[:, 0, 0:4, :],
                in1=cst[:, 0, :].unsqueeze(1).to_broadcast([128, 4, 128]), op=ALU.mult),
                reads=[t_stage, t_cst], writes=[t_wsT])
        P.dma("sp", stage, biasT_d, writes=[t_stage])
        for g in range(2):
            for e2 in range(2):
                P.op("dve", lambda e: e.tensor_scalar(out=bias8[:, g, e2, :, :, :],
                                                      in0=stage[:, :, 4 * g + e2:4 * g + 4:2, :],
                                                      scalar1=8.0, scalar2=None, op0=ALU.mult),
                     reads=[t_stage], writes=[t_bias8])
        for l in range(nl):
            P.op("act", lambda e: e.activation(out=esink[:, l, :], in_=rowpar[:, l, 512:520],
                                               func=AF.Exp),
                 reads=[t_rowpar], writes=[t_esink])
        P.op("dve", lambda e: e.tensor_scalar(out=esink[:], in0=esink[:], scalar1=float(np.exp(-SOFT_C)),
                                              scalar2=None, op0=ALU.mult),
             reads=[t_esink], writes=[t_esink])
        ltmp = sb("ltmp", [128, 8]); t_ltmp = T("ltmp")
        if nl == 2:
            P.op("dve", lambda e: e.tensor_tensor(out=ltmp[:, 0:2], in0=colpar[:, 0, 20:22],
                                                  in1=colpar[:, 0, 18:20], op=ALU.subtract),
                 reads=[t_colpar], writes=[t_ltmp])
            P.op("act", lambda e: e.activation(out=ltmp[:, 2:4], in_=ltmp[:, 0:2], func=AF.Sigmoid),
                 reads=[t_ltmp], writes=[t_ltmp])
        P.op("dve", lambda e: e.memset(lbp[:], 0.0), writes=[t_lbp])
        for pr in range(2):
            P.op("dve", lambda e: e.memset(lbp[:, 0, pr, 0:1], 1.0), reads=[t_lbp], writes=[t_lbp])
            if nl == 2:
                P.op("dve", lambda e: e.tensor_copy(out=lbp[:, 1, pr, 1:2], in_=ltmp[:, 2 + pr:3 + pr]),
                     reads=[t_ltmp, t_lbp], writes=[t_lbp])
                P.op("dve", lambda e: e.tensor_scalar(out=lbp[:, 1, pr, 0:1], in0=ltmp[:, 2 + pr:3 + pr],
                                                      scalar1=-1.0, scalar2=1.0, op0=ALU.mult, op1=ALU.add),
                     reads=[t_ltmp, t_lbp], writes=[t_lbp])

        t_sc = {}
        for l in range(nl):
            for nm in ("fm", "tm", "wo", "gu", "wd"):
                t_sc[(nm, l)] = T("sc_%s%d" % (nm, l))

        def cast(dst, src, t):
            P.dma("pool", dst, src, writes=[t], add=True)

        def emit_casts():
            for l in range(nl):
                for dc in range(8):
                    rows = slice(dc * 128, (dc + 1) * 128)
                    cast(winfm_s[l, :, dc, :], win_d[l, rows, NTM:NTM + NFM], t_sc[("fm", l)])
                for dc in range(8):
                    rows = slice(dc * 128, (dc + 1) * 128)
                    cast(wintm_s[l, :, dc, :], win_d[l, rows, 0:NTM], t_sc[("tm", l)])
                for dc in range(8):
                    rows = slice(dc * 128, (dc + 1) * 128)
                    cast(wout_s[l, :, :, dc, :].rearrange("h p c -> p h c"),
                         wout_d[l, rows, :].rearrange("p (h c) -> p h c", c=512), t_sc[("wo", l)])
                for dc in range(8):
                    rows = slice(dc * 128, (dc + 1) * 128)
                    cast(wg_s[l, :, :, dc, :].rearrange("g p c -> p g c"),
                         wg_d[l, rows, :].rearrange("p (g c) -> p g c", c=256), t_sc[("gu", l)])
                    cast(wu_s[l, :, :, dc, :].rearrange("g p c -> p g c"),
                         wu_d[l, rows, :].rearrange("p (g c) -> p g c", c=256), t_sc[("gu", l)])
                for pc in range(6):
                    ncc = 4 if pc < 5 else 2
                    cast(wd_s[l, pc, :, 0:ncc, :],
                         wd_d[l, pc * 512:pc * 512 + ncc * 128, :].rearrange("(cc p) d -> p cc d", p=128),
                         t_sc[("wd", l)])

        xres = sb("xres", [128, NB, D]); t_x = [T("x%d" % b) for b in range(NB)]
        hT = sb("hT", [128, 8, 2 + TT], BF16); t_hT = T("hT")
        halo = sb("halo", [128, nl, 8, 2], BF16); t_halo = [T("halo%d" % l) for l in range(nl)]
        wfm = sb("wfm", [128, 8, NFM], BF16); t_wfm = T("wfm")
        wtm = sb("wtm", [128, 8, NTM], BF16); t_wtm = T("wtm")
        ring = sb("ring", [128, NRING, 4096], BF16); t_ring = [T("ring%d" % i) for i in range(NRING)]
        aT = sb("aT", [128, NCH, TT], BF16); t_aT = T("aT")
        qT = sb("qT", [128, 4, TT], BF16); t_qT = T("qT")
        kT2 = sb("kT2", [128, nl, 2, 128 + TT], BF16); t_kT2 = [T("kT2%d" % l) for l in range(nl)]
        vaug = sb("vaug", [128, nl, NB + 1, 2, 65], BF16); t_vaug = [T("vaug%d" % l) for l in range(nl)]
        Sst = sb("Sst", [128, nl, 2, HN + 1, 64]); t_S = [T("S%d" % l) for l in range(nl)]
        sqb = sb("sqb", [128, 2, TT]); t_sqb = [T("sqb0"), T("sqb1")]
        epsc = sb("epsc", [128, 1]); t_epsc = T("epsc")
        nstat = sb("nstat", [128, 8]); t_nstat = T("nstat")
        hs = sb("hs", [128, 2, D], BF16); t_hs = [T("hs0"), T("hs1")]
        u4 = sb("u4", [128, NB, 256], BF16); t_u4 = T("u4")
        vv4 = sb("vv4", [128, NB, 256], BF16); t_vv4 = T("vv4")
        vh4 = sb("vh4", [128, NB, 256], BF16); t_vh4 = T("vh4")
        sg4 = sb("sg4", [128, NB, 256], BF16); t_sg4 = T("sg4")
        vtmp = sb("vtmp", [128, 2, 256]); t_vtmp = [T("vtmp0"), T("vtmp1")]
        vsq = sb("vsq", [128, 2, 256]); t_vsq = [T("vsq0"), T("vsq1")]
        vstat = sb("vstat", [128, 2, NB * 4]); t_vstat = T("vstat")
        rst2v = [vtmp[:].rearrange("p a c -> p (a c)")[0:2, :], vsq[:].rearrange("p a c -> p (a c)")[0:2, :]]
        t_rst2 = [t_vtmp, t_vsq]
        vn = sb("vn", [128, 256], BF16); t_vn = T("vn")
        vexp = sb("vexp", [128, 4, HN, 64], BF16); t_vexp = T("vexp")
        mixed2 = sb("mixed", [128, 2, D], BF16); t_mixed2 = [T("mixed0"), T("mixed1")]
        mixT = sb("mixT", [128, 8, 128], BF16); t_mixT = T("mixT")
        junk = mixT[:].rearrange("p c t -> p (c t)"); t_junk = t_mixT
        PT = sb("PT", [128, 2, 4, 128], BF16); t_PT = [T("PT0"), T("PT1")]
        hg = {}
        for nm in ("A", "B", "C", "E"):
            hg[nm] = (sb("hg_" + nm, [128, 2, 128]), T("hg_" + nm))
        for nm in ("qd", "kd", "kl"):
            hg[nm] = (sb("hg_" + nm, [128, 2, 2, 128], BF16), [T("hg_" + nm + "0"), T("hg_" + nm + "1")])
        dec2 = sb("dec", [128, 2, 2, HN]); t_dec2 = [T("dec0"), T("dec1")]
        kltok = sb("kltok", [128, 2, 2, 128], BF16); t_kltok = T("kltok")
        attnT = sb("attnT", [128, 4, 128], BF16); t_attnT = T("attnT")
        Qm = sb("Qm", [128, 2, HN, 128], BF16); t_Qm = T("Qm")
        Sbf = sb("Sbf", [128, 2, HN, 64], BF16); t_Sbf = T("Sbf")
        osq_t = vsq; t_osq = t_vsq[1]
        yc = sb("yc", [128, 256]); t_yc = T("yc")
        ostat = sb("ostat", [128, 8]); t_ostat = T("ostat")
        den = sb("den", [128, 2, 8]); t_den = [T("den0"), T("den1")]
        f1 = sb("f1", [128, 2, 256]); t_f1 = [T("f1a"), T("f1b")]
        f2 = sb("f2", [128, 2, 256]); t_f2 = [T("f2a"), T("f2b")]
        pbs = [st.enter_context(nc.psum_tensor("pb%d" % i, [128, 512], F32)) for i in range(8)]
        t_pb = [T("pb%d" % i) for i in range(8)]
        bank_i = [0]

        def bank():
            i = bank_i[0] % 8
            bank_i[0] += 1
            return pbs[i], t_pb[i]

        P.op("pool", lambda e: e.memset(epsc[:], EPS), writes=[t_epsc])
        P.op("pool", lambda e: e.memset(Qm[:], 0.0), writes=[t_Qm])
        P.op("pool", lambda e: e.memset(kltok[:], 0.0), writes=[t_kltok])
        P.op("pool", lambda e: e.memset(Sst[:], 0.0), writes=t_S)
        P.op("pool", lambda e: e.memset(halo[:], 0.0), writes=t_halo)
        P.op("pool", lambda e: e.memset(kT2[:], 0.0), writes=t_kT2)
        P.op("pool", lambda e: e.memset(vaug[:], 0.0), writes=t_vaug)
        for l in range(nl):
            P.op("pool", lambda e: e.memset(vaug[:, l, :, :, 64:65], 1.0), reads=[t_vaug[l]], writes=[t_vaug[l]])

        emit_casts()

        items = []
        for ti in range(ntile):
            for l in range(nl):
                for hf in range(2):
                    items.append([(wout_s[l, hf].rearrange("p a b -> p (a b)"), 0, 4096, t_sc[("wo", l)])])
                for g in range(11):
                    items.append([(wg_s[l, g].rearrange("p a b -> p (a b)"), 0, 2048, t_sc[("gu", l)]),
                                  (wu_s[l, g].rearrange("p a b -> p (a b)"), 2048, 2048, t_sc[("gu", l)])])
                for pc in range(6):
                    nv = 4096 if pc < 5 else 2048
                    items.append([(wd_s[l, pc].rearrange("p a b -> p (a b)")[:, 0:nv], 0, nv, t_sc[("wd", l)])])
        rstate = {"issued": 0, "consumed": 0, "released": 0}

        def ring_fill():
            while rstate["issued"] < len(items) and rstate["issued"] < rstate["released"] + NRING:
                j = rstate["issued"]
                s = j % NRING
                for k, (src, off, n, tsrc) in enumerate(items[j]):
                    P.dma("sp", ring[:, s, off:off + n], src, reads=[tsrc], writes=[t_ring[s]], add=(k > 0))
                rstate["issued"] += 1

        def ring_done(k=1):
            rstate["released"] += k
            ring_fill()

        def ring_next():
            j = rstate["consumed"]
            assert j < rstate["issued"]
            rstate["consumed"] += 1
            s = j % NRING
            return ring[:, s, :], t_ring[s]

        cp = lambda l, a, b: colpar[:, l, a:b]

        dbgs = {}

        def dbg(name, ap, t, shape):
            if not debug or name in dbgs:
                return
            d = nc.dram_tensor("dbg_" + name, list(shape), ap.dtype, kind="ExternalOutput").ap()
            P.dma("sp", d, ap, reads=[t], writes=[T("dbgo_" + name)])
            dbgs[name] = True

        def ck(k):
            P.mark("ck%s" % k)
            if stop is not None and abs(stop - k) < 1e-6:
                raise _Stop()

        def rstd_act(out, in_, n, t_in, t_out):
            t_in = t_in if isinstance(t_in, list) else [t_in]
            t_out = t_out if isinstance(t_out, list) else [t_out]
            P.op("act", lambda e: e.activation(out=out, in_=in_, func=AF.Ln, scale=1.0 / n,
                                               bias=epsc[0:out.shape[0], :]),
                 reads=t_in + [t_epsc], writes=t_out)
            P.op("act", lambda e: e.activation(out=out, in_=out, func=AF.Exp, scale=-0.5),
                 reads=t_out, writes=t_out)

        def rmsnorm_to_hT(l, gcol0):
            for b in range(NB):
                P.op("act", lambda e: e.activation(out=junk, in_=xres[:, b, :], func=AF.Square,
                                                   accum_out=nstat[:, b:b + 1]),
                     reads=[t_x[b]], writes=[t_junk, t_nstat])
            rstd_act(nstat[:, 4:8], nstat[:, 0:4], D, t_nstat, t_nstat)
            for b in range(NB):
                s_ = b % 2
                P.op("act", lambda e: e.activation(out=hs[:, s_, :], in_=xres[:, b, :], func=AF.Identity,
                                                   scale=nstat[:, 4 + b:5 + b]),
                     reads=[t_x[b], t_nstat], writes=[t_hs[s_]])
                pb, tpb = bank()
                pbb = pb[:].bitcast(BF16)
                for dc in range(8):
                    P.op("pe", lambda e: e.transpose(out=pbb[:, dc * 128:(dc + 1) * 128],
                                                     in_=hs[:, s_, dc * 128:(dc + 1) * 128], identity=ident[:]),
                         reads=[t_hs[s_], t_ident], writes=[tpb])
                P.op("dve", lambda e: e.tensor_tensor(
                    out=hT[:, :, 2 + b * 128:2 + (b + 1) * 128],
                    in0=pbb.rearrange("p (c t) -> p c t", t=128),
                    in1=cp(l, gcol0, gcol0 + 8).unsqueeze(2).to_broadcast([128, 8, 128]), op=ALU.mult),
                    reads=[tpb, t_colpar], writes=[t_hT])

        hTv = hT[:, :, 2:2 + TT]

        def fm_phase(l):
            def main_mm(ft):
                pb, tpb = bank()
                for dc in range(8):
                    P.op("pe", lambda e: e.matmul(pb[:], lhsT=wfm[:, dc, ft * 128:(ft + 1) * 128],
                                                  rhs=hTv[:, dc, :], start=(dc == 0), stop=(dc == 7)),
                         reads=[t_wfm, t_hT], writes=[tpb])
                return pb, tpb

            def dst_of(ft):
                if ft < 4:
                    return qT[:, ft, :], t_qT, 16
                return kT2[:, l, ft - 4, 128:128 + TT], t_kT2[l], 17

            def stage_a(ft):
                pb, tpb = main_mm(ft)
                dst, tdst, _ = dst_of(ft)
                s_ = ft % 2
                P.op("act", lambda e: e.activation(out=dst, in_=pb[:], func=AF.Copy), reads=[tpb], writes=[tdst])
                P.op("act", lambda e: e.activation(out=sqb[:, s_, :], in_=pb[:], func=AF.Square),
                     reads=[tpb], writes=[t_sqb[s_]])

            def stage_b(ft):
                s_ = ft % 2
                pb2, tpb2 = bank()
                P.op("pe", lambda e: e.matmul(pb2[0:2, :], lhsT=blk2[:], rhs=sqb[:, s_, :], start=True, stop=True),
                     reads=[t_blk, t_sqb[s_]], writes=[tpb2])
                rstd_act(rst2v[s_], pb2[0:2, :], 64, tpb2, t_rst2[s_])

            def stage_c(ft):
                s_ = ft % 2
                dst, tdst, gc = dst_of(ft)
                pb3, tpb3 = bank()
                P.op("pe", lambda e: e.matmul(pb3[:], lhsT=blk2T[:], rhs=rst2v[s_], start=True, stop=True),
                     reads=[t_blk] + t_rst2[s_], writes=[tpb3])
                P.op("dve", lambda e: e.scalar_tensor_tensor(out=dst, in0=pb3[:], scalar=cp(l, gc, gc + 1),
                                                             in1=dst, op0=ALU.mult, op1=ALU.mult),
                     reads=[tpb3, tdst, t_colpar], writes=[tdst])

            for step in range(8):
                if step < 6:
                    stage_a(step)
                if 0 <= step - 1 < 6:
                    stage_b(step - 1)
                if 0 <= step - 2 < 6:
                    stage_c(step - 2)
                if step == 5:
                    for ft in (8, 9):
                        pb, tpb = main_mm(ft)
                        P.op("act", lambda e: e.activation(out=cqf[:, ft - 6, :], in_=pb[:], func=AF.Copy),
                             reads=[tpb], writes=[t_cqf])
            for ft in (6, 7):
                pb, tpb = main_mm(ft)
                P.op("act", lambda e: e.activation(out=cqf[:, ft - 6, :], in_=pb[:], func=AF.Silu),
                     reads=[tpb], writes=[t_cqf])

        def tm_phase(l):
            def grp(b, c0, n):
                pb, tpb = bank()
                cols = slice(b * 128, (b + 1) * 128)
                for dc in range(8):
                    P.op("pe", lambda e: e.matmul(pb[:, 0:n], lhsT=hTv[:, dc, cols], rhs=wtm[:, dc, c0:c0 + n],
                                                  start=(dc == 0), stop=(dc == 7)),
                         reads=[t_hT, t_wtm], writes=[tpb])
                return pb, tpb
            for b in range(NB):
                p2, tp2 = grp(b, 512, 512)
                P.op("act", lambda e: e.activation(out=sg4[:, b, :], in_=p2[:, 256:512], func=AF.Silu),
                     reads=[tp2], writes=[t_sg4])
                P.op("act", lambda e: e.activation(out=vh4[:, b, :], in_=p2[:, 0:256], func=AF.Copy),
                     reads=[tp2], writes=[t_vh4])
            for b in range(NB):
                p3, tp3 = grp(b, 1024, 128)
                P.op("act", lambda e: e.activation(
                    out=vaug[:, l, b + 1, :, 0:64], in_=p3[:, 0:128].rearrange("p (g d) -> p g d", d=64),
                    func=AF.Copy), reads=[tp3], writes=[t_vaug[l]])
            for b in range(NB):
                p1, tp1 = grp(b, 0, 512)
                s_ = b % 2
                P.op("act", lambda e: e.activation(out=u4[:, b, :], in_=p1[:, 0:256], func=AF.Gelu),
                     reads=[tp1], writes=[t_u4])
                P.op("act", lambda e: e.activation(out=vtmp[:, s_, :], in_=p1[:, 256:512], func=AF.Gelu),
                     reads=[tp1], writes=[t_vtmp[s_]])
                P.op("pool", lambda e: e.tensor_tensor(out=vsq[:, s_, :], in0=vtmp[:, s_, :], in1=vtmp[:, s_, :], op=ALU.mult),
                     reads=[t_vtmp[s_]], writes=[t_vsq[s_]])
                P.op("dve", lambda e: e.tensor_reduce(out=vstat[:, 0, b * 4:(b + 1) * 4],
                                                      in_=vsq[:, s_, :].rearrange("p (g d) -> p g d", d=64),
                                                      axis=AX.X, op=ALU.add),
                     reads=[t_vsq[s_]], writes=[t_vstat])
                P.op("pool", lambda e: e.tensor_copy(out=vv4[:, b, :], in_=vtmp[:, s_, :]),
                     reads=[t_vtmp[s_]], writes=[t_vv4])

        def chain_a(l, b):
            mixed = mixed2[:, b % 2, :]; t_mixed = t_mixed2[b % 2]
            for g in range(4):
                P.op("dve", lambda e: e.scalar_tensor_tensor(
                    out=vn[:, g * 64:(g + 1) * 64], in0=vv4[:, b, g * 64:(g + 1) * 64],
                    scalar=vstat[:, 1, b * 4 + g:b * 4 + g + 1], in1=rowpar[:, l, g * 64:(g + 1) * 64],
                    op0=ALU.mult, op1=ALU.mult),
                    reads=[t_vv4, t_vstat, t_rowpar], writes=[t_vn])
            yield
            pa, tpa = bank()
            for g in range(4):
                P.op("pe", lambda e: e.matmul(pa[:, g * 64:(g + 1) * 64], lhsT=wsT[:, l, g, :],
                                              rhs=vn[:, g * 64:(g + 1) * 64], start=True, stop=True),
                     reads=[t_wsT, t_vn], writes=[tpa])
            yield
            for g in range(4):
                P.op("dve", lambda e: e.scalar_tensor_tensor(
                    out=mixed[:, g * 64:(g + 1) * 64], in0=pa[:, g * 64:(g + 1) * 64],
                    scalar=bT[:, l, g:g + 1], in1=u4[:, b, g * 64:(g + 1) * 64], op0=ALU.add, op1=ALU.mult),
                    reads=[tpa, t_bT, t_u4], writes=[t_mixed])
                if g % 2 == 1:
                    yield

        def chain_b(l, b, gb):
            mixed = mixed2[:, b % 2, :]; t_mixed = t_mixed2[b % 2]
            cols = slice(b * 128, (b + 1) * 128)
            kbs = [1] if gb == 0 else [0, 1]
            for g in range(2):
                for e2 in range(2):
                    ps_, tps = bank()
                    P.op("pe", lambda e: e.matmul(ps_[:], lhsT=ident[:],
                                                  rhs=bias8[:, g, e2, :, :, :].rearrange("p k r i -> p (k r i)"),
                                                  start=True, stop=False),
                         reads=[t_ident, t_bias8], writes=[tps])
                    pr_ = slice(64 * e2, 64 * e2 + 64)
                    nmm = len(kbs) * 2
                    imm = 0
                    for kb in kbs:
                        for rp in range(2):
                            h = 4 * g + 2 * rp + e2
                            m = h // 2
                            kc = slice(b * 128 + kb * 128, b * 128 + kb * 128 + 128)
                            imm += 1
                            P.op("pe", lambda e: e.matmul(ps_[:, (kb * 2 + rp) * 128:(kb * 2 + rp + 1) * 128],
                                                          lhsT=kT2[pr_, l, g, kc], rhs=qT[pr_, m, cols],
                                                          start=False, stop=(imm == nmm)),
                                 reads=[t_kT2[l], t_qT], writes=[tps])
                    yield
                    P.op("act", lambda e: e.activation(out=PT[:, :, e2:4:2, :],
                                                       in_=ps_[:].rearrange("p (k r i) -> p k r i", k=2, r=2),
                                                       func=AF.Exp, scale=0.125),
                         reads=[tps], writes=[t_PT[0], t_PT[1]])
                    yield
                po, tpo = bank()
                for r in range(4):
                    for kb in kbs:
                        P.op("pe", lambda e: e.matmul(po[:, r * 65:(r + 1) * 65], lhsT=PT[:, kb, r, :],
                                                      rhs=vaug[:, l, b + kb, g, :], start=(kb == kbs[0]),
                                                      stop=(kb == 1)),
                             reads=[t_PT[kb], t_vaug[l]], writes=[tpo])
                yield
                pov = po[:, 0:260].rearrange("p (r d) -> p r d", d=65)
                P.op("dve", lambda e: e.tensor_tensor(out=den[:, g, 0:4].unsqueeze(2), in0=pov[:, :, 64:65],
                                                      in1=esink[:, l, 4 * g:4 * g + 4].unsqueeze(2), op=ALU.add),
                     reads=[tpo, t_esink], writes=[t_den[g]])
                yield
                P.op("dve", lambda e: e.reciprocal(out=den[:, g, 4:8], in_=den[:, g, 0:4]),
                     reads=[t_den[g]], writes=[t_den[g]])
                yield
                P.op("dve", lambda e: e.tensor_tensor(
                    out=mixed[:, 256 + g * 256:256 + (g + 1) * 256].rearrange("p (r d) -> p r d", d=64),
                    in0=pov[:, :, 0:64], in1=den[:, g, 4:8].unsqueeze(2).to_broadcast([128, 4, 64]), op=ALU.mult),
                    reads=[tpo, t_den[g]], writes=[t_mixed])
                yield

        def chain_c1(l, b):
            cols = slice(b * 128, (b + 1) * 128)
            pp = b % 2
            A, tA = hg["A"]; Bq, tB = hg["B"]; C, tC = hg["C"]; E, tE = hg["E"]
            qd, tqd = hg["qd"][0][:, pp], hg["qd"][1][pp]
            kd, tkd = hg["kd"][0][:, pp], hg["kd"][1][pp]
            kl, tkl = hg["kl"][0][:, pp], hg["kl"][1][pp]
            dec, t_dec = dec2[:, pp], t_dec2[pp]
            q_ = cqf[:, 0:2, cols]
            zf = cqf[:, 2:4, cols]
            P.op("act", lambda e: e.activation(out=A[:], in_=zf, func=AF.Exp, scale=-1.0), reads=[t_cqf], writes=[tA])
            yield
            P.op("act", lambda e: e.activation(out=A[:], in_=A[:], func=AF.Ln, bias=1.0), reads=[tA], writes=[tA])
            yield
            P.op("act", lambda e: e.activation(out=A[:], in_=A[:], func=AF.Exp, scale=-1.0), reads=[tA], writes=[tA])
            yield
            for pr in range(2):
                P.op("dve", lambda e: e.tensor_scalar(out=Bq[:, pr, :], in0=A[:, pr, :],
                                                      scalar1=lbp[:, l, pr, 0:1], scalar2=lbp[:, l, pr, 1:2],
                                                      op0=ALU.mult, op1=ALU.add),
                     reads=[tA, t_lbp], writes=[tB])
            yield
            P.op("act", lambda e: e.activation(out=A[:], in_=Bq[:], func=AF.Ln), reads=[tB], writes=[tA])
            yield
            for pr in range(2):
                P.op("dve", lambda e: e.tensor_tensor_scan(out=C[:, pr, :], data0=cst[:, 2, :],
                                                           data1=A[:, pr, :], initial=0.0,
                                                           op0=ALU.mult, op1=ALU.add),
                     reads=[tA, t_cst], writes=[tC])
            yield
            P.op("pool", lambda e: e.tensor_scalar(out=Bq[:], in0=Bq[:], scalar1=-1.0, scalar2=1.0,
                                                   op0=ALU.mult, op1=ALU.add), reads=[tB], writes=[tB])
            yield
            P.op("act", lambda e: e.activation(out=A[:], in_=C[:], func=AF.Exp), reads=[tC], writes=[tA])
            P.op("act", lambda e: e.activation(out=E[:], in_=C[:], func=AF.Exp, scale=-1.0), reads=[tC], writes=[tE])
            cum4 = C[:].rearrange("p a (n j) -> p (a n) j", j=HCH)
            P.op("act", lambda e: e.activation(out=dec.rearrange("p a n -> p (a n)").unsqueeze(2),
                                               in_=cum4[:, :, HCH - 1:HCH], func=AF.Exp),
                 reads=[tC], writes=[t_dec])
            yield
            P.op("dve", lambda e: e.tensor_tensor(out=qd, in0=q_, in1=A[:], op=ALU.mult),
                 reads=[t_cqf, tA], writes=[tqd])
            yield
            P.op("pool", lambda e: e.tensor_tensor(out=kd, in0=Bq[:], in1=E[:], op=ALU.mult),
                 reads=[tB, tE], writes=[tkd])
            yield
            P.op("dve", lambda e: e.tensor_tensor(
                out=A[:].rearrange("p a (n j) -> p (a n) j", j=HCH),
                in0=cum4[:, :, HCH - 1:HCH].to_broadcast([128, 2 * HN, HCH]), in1=cum4, op=ALU.subtract),
                reads=[tC, tA], writes=[tA])
            yield
            P.op("act", lambda e: e.activation(out=A[:], in_=A[:], func=AF.Exp), reads=[tA], writes=[tA])
            yield
            P.op("pool", lambda e: e.tensor_tensor(out=kl, in0=Bq[:], in1=A[:], op=ALU.mult),
                 reads=[tB, tA], writes=[tkl])
            yield

        def emit_vexp(b):
            for h in range(4):
                P.op("pool", lambda e: e.tensor_tensor(
                    out=vexp[:, h, :, :], in0=vh4[:, b, h * 64:(h + 1) * 64].unsqueeze(1).to_broadcast([128, HN, 64]),
                    in1=cm[:].unsqueeze(2).to_broadcast([128, HN, 64]), op=ALU.mult),
                    reads=[t_vh4, t_cm], writes=[t_vexp])

        def chain_c2(l, b):
            pp = b % 2
            mixed = mixed2[:, pp, :]; t_mixed = t_mixed2[pp]
            qd, tqd = hg["qd"][0][:, pp], hg["qd"][1][pp]
            kd, tkd = hg["kd"][0][:, pp], hg["kd"][1][pp]
            kl, tkl = hg["kl"][0][:, pp], hg["kl"][1][pp]
            dec, t_dec = dec2[:, pp], t_dec2[pp]
            if b == 0:
                emit_vexp(0)
                yield
            for pr in range(2):
                v = Qm[:, pr, :, :]
                dst = bass.AP(v.tensor, v.offset, [list(v.ap[0]), [128 + HCH, HN], [1, HCH]])
                P.op("pool", lambda e: e.tensor_copy(out=dst, in_=qd[:, pr, :].rearrange("p (n j) -> p n j", j=HCH)),
                     reads=[tqd], writes=[t_Qm])
            for e2 in range(2):
                pat, tpat = bank()
                rows = slice(64 * e2, 64 * e2 + 64)
                for pr in range(2):
                    P.op("pe", lambda e: e.matmul(pat[:, pr * 128:(pr + 1) * 128], lhsT=kd[rows, pr, :],
                                                  rhs=qd[rows, pr, :], start=True, stop=True),
                         reads=[tkd, tqd], writes=[tpat])
                yield
                P.op("dve", lambda e: e.tensor_tensor(
                    out=attnT[:, e2:4:2, :], in0=pat[:, 0:256].rearrange("p (h t) -> p h t", t=128),
                    in1=cst[:, 1, :].unsqueeze(1).to_broadcast([128, 2, 128]), op=ALU.mult),
                    reads=[tpat, t_cst], writes=[t_attnT])
                yield
            pk, tpk = bank()
            pkb = pk[:].bitcast(BF16)
            for pr in range(2):
                P.op("pe", lambda e: e.transpose(out=pkb[:, pr * 128:(pr + 1) * 128], in_=kl[:, pr, :],
                                                 identity=ident[:]),
                     reads=[tkl, t_ident], writes=[tpk])
            yield
            for pr in range(2):
                v = kltok[:, pr, :, :]
                dstk = bass.AP(v.tensor, v.offset, [list(v.ap[0]), [192, 2], [1, 64]])
                P.op("act", lambda e: e.activation(out=dstk, in_=pkb[:, pr * 128:(pr + 1) * 128].rearrange("p (a k) -> p a k", k=64),
                                                   func=AF.Copy),
                     reads=[tpk], writes=[t_kltok])
            yield
            pds = []
            for pr in range(2):
                pd_, tpd = bank()
                for e2 in range(2):
                    h = 2 * pr + e2
                    P.op("pe", lambda e: e.matmul(pd_[:, 0:HN * 64], lhsT=kltok[:, pr, e2, :],
                                                  rhs=vexp[:, h, :, :].rearrange("p n v -> p (n v)"),
                                                  start=(e2 == 0), stop=(e2 == 1)),
                         reads=[t_kltok, t_vexp], writes=[tpd])
                pds.append((pd_, tpd))
                yield
            if b + 1 < NB:
                emit_vexp(b + 1)
            for n in range(HN):
                for pr in range(2):
                    pd_, tpd = pds[pr]
                    P.op("dve", lambda e: e.scalar_tensor_tensor(
                        out=Sst[:, l, pr, n + 1, :], in0=Sst[:, l, pr, n, :], scalar=dec[:, pr, n:n + 1],
                        in1=pd_[:, n * 64:(n + 1) * 64], op0=ALU.mult, op1=ALU.add),
                        reads=[t_S[l], t_dec, tpd], writes=[t_S[l]])
                yield
            P.op("act", lambda e: e.activation(out=Sbf[:], in_=Sst[:, l, :, 0:HN, :], func=AF.Copy),
                 reads=[t_S[l]], writes=[t_Sbf])
            P.op("pool", lambda e: e.tensor_copy(out=Sst[:, l, :, 0, :], in_=Sst[:, l, :, HN, :]),
                 reads=[t_S[l]], writes=[t_S[l]])
            yield
            pqs = []
            for e2 in range(2):
                pq, tpq = bank()
                rows = slice(64 * e2, 64 * e2 + 64)
                for pr in range(2):
                    h = 2 * pr + e2
                    P.op("pe", lambda e: e.matmul(pq[:, pr * 64:(pr + 1) * 64], lhsT=attnT[:, h, :],
                                                  rhs=vh4[:, b, h * 64:(h + 1) * 64], start=True, stop=False),
                         reads=[t_attnT, t_vh4], writes=[tpq])
                    for n in range(HN):
                        P.op("pe", lambda e: e.matmul(pq[:, pr * 64:(pr + 1) * 64], lhsT=Qm[rows, pr, n, :],
                                                      rhs=Sbf[rows, pr, n, :], start=False, stop=(n == HN - 1)),
                             reads=[t_Qm, t_Sbf], writes=[tpq])
                    yield
                pqs.append((pq, tpq))

            def hv(t, e2):
                return t.rearrange("p (a e d) -> p a e d", e=2, d=64)[:, :, e2, :]
            for e2 in range(2):
                pq, tpq = pqs[e2]
                pq3 = pq[:, 0:128].rearrange("p (a d) -> p a d", d=64)
                P.op("act", lambda e: e.activation(out=hv(vsq[:, 1, :], e2), in_=pq3, func=AF.Square),
                     reads=[tpq], writes=[t_osq])
                P.op("dve", lambda e: e.tensor_reduce(out=ostat[:, 2 * e2:2 * e2 + 2], in_=hv(vsq[:, 1, :], e2), axis=AX.X, op=ALU.add),
                     reads=[t_osq], writes=[t_ostat])
                yield
            rstd_act(ostat[:, 4:8], ostat[:, 0:4], 64, t_ostat, t_ostat)
            yield
            for e2 in range(2):
                pq, tpq = pqs[e2]
                for pr in range(2):
                    h = 2 * pr + e2
                    P.op("dve", lambda e: e.scalar_tensor_tensor(
                        out=mixed[:, 768 + h * 64:768 + (h + 1) * 64], in0=pq[:, pr * 64:(pr + 1) * 64],
                        scalar=ostat[:, 4 + 2 * e2 + pr:5 + 2 * e2 + pr], in1=sg4[:, b, h * 64:(h + 1) * 64],
                        op0=ALU.mult, op1=ALU.mult),
                        reads=[tpq, t_ostat, t_sg4], writes=[t_mixed])
                yield

        def interleave(gens):
            ent = []
            for g in gens:
                if isinstance(g, tuple):
                    ent.append([g[0], g[1], g[2]])
                else:
                    ent.append([g, 1, 0])
            rnd = 0
            while ent:
                for e_ in list(ent):
                    if e_[2] > rnd:
                        continue
                    for _ in range(e_[1]):
                        try:
                            next(e_[0])
                        except StopIteration:
                            ent.remove(e_)
                            break
                rnd += 1

        def out_proj(b, wo):
            mixed = mixed2[:, b % 2, :]; t_mixed = t_mixed2[b % 2]
            pm_, tpm = bank()
            yield
            pmb = pm_[:].bitcast(BF16)
            for dc in range(8):
                P.op("pe", lambda e: e.transpose(out=pmb[:, dc * 128:(dc + 1) * 128],
                                                 in_=mixed[:, dc * 128:(dc + 1) * 128], identity=ident[:]),
                     reads=[t_mixed, t_ident], writes=[tpm])
            yield
            P.op("act", lambda e: e.activation(out=mixT[:].rearrange("p c t -> p (c t)"), in_=pmb, func=AF.Copy),
                 reads=[tpm], writes=[t_mixT])
            yield
            for hf in range(2):
                po2, tpo2 = bank()
                wsl, twsl = wo[hf]
                for dc in range(8):
                    P.op("pe", lambda e: e.matmul(po2[:], lhsT=mixT[:, dc, :], rhs=wsl[:, dc * 512:(dc + 1) * 512],
                                                  start=(dc == 0), stop=(dc == 7)),
                         reads=[t_mixT, twsl], writes=[tpo2])
                P.op("dve", lambda e: e.tensor_tensor(out=xres[:, b, hf * 512:(hf + 1) * 512], in0=po2[:],
                                                      in1=xres[:, b, hf * 512:(hf + 1) * 512], op=ALU.add),
                     reads=[tpo2, t_x[b]], writes=[t_x[b]])
                yield

        def ffn(l):
            units = [(g, cc, sub) for g in range(11) for cc in range(2) for sub in range(2)]
            pend = None
            gu = tgu = None

            def stage2(p_):
                c, sub, s_, pup, tpup = p_
                P.op("act", lambda e: e.activation(out=f2[:, s_, :], in_=f1[:, s_, :], func=AF.Silu),
                     reads=[t_f1[s_]], writes=[t_f2[s_]])
                P.op("dve", lambda e: e.tensor_tensor(out=aT[:, c, sub * 256:(sub + 1) * 256], in0=f2[:, s_, :],
                                                      in1=pup[:, 0:256], op=ALU.mult),
                     reads=[t_f2[s_], tpup], writes=[t_aT])

            for k, (g, cc, sub) in enumerate(units):
                if cc == 0 and sub == 0:
                    if g > 0:
                        ring_done(1)
                    ring_fill()
                    gu, tgu = ring_next()
                c = 2 * g + cc
                cw = 22 + 4 * c
                pgt, tpgt = bank()
                pup, tpup = bank()
                for dc in range(8):
                    P.op("pe", lambda e: e.matmul(pgt[:, 0:258], lhsT=gu[:, dc * 256 + cc * 128: dc * 256 + (cc + 1) * 128],
                                                  rhs=hT[:, dc, sub * 256: sub * 256 + 258],
                                                  start=(dc == 0), stop=(dc == 7)),
                         reads=[tgu, t_hT], writes=[tpgt])
                for dc in range(8):
                    P.op("pe", lambda e: e.matmul(pup[:, 0:256], lhsT=gu[:, 2048 + dc * 256 + cc * 128: 2048 + dc * 256 + (cc + 1) * 128],
                                                  rhs=hT[:, dc, 2 + sub * 256: 2 + sub * 256 + 256],
                                                  start=(dc == 0), stop=(dc == 7)),
                         reads=[tgu, t_hT], writes=[tpup])
                s_ = k % 2
                P.op("act", lambda e: e.activation(out=f1[:, s_, :], in_=pgt[:, 2:258], func=AF.Identity,
                                                   scale=cp(l, cw + 2, cw + 3), bias=cp(l, cw + 3, cw + 4)),
                     reads=[tpgt, t_colpar], writes=[t_f1[s_]])
                P.op("dve", lambda e: e.scalar_tensor_tensor(out=f2[:, s_, :], in0=pgt[:, 1:257],
                                                             scalar=cp(l, cw + 1, cw + 2), in1=f1[:, s_, :],
                                                             op0=ALU.mult, op1=ALU.add),
                     reads=[tpgt, t_colpar, t_f1[s_]], writes=[t_f2[s_]])
                P.op("dve", lambda e: e.scalar_tensor_tensor(out=f1[:, s_, :], in0=pgt[:, 0:256],
                                                             scalar=cp(l, cw, cw + 1), in1=f2[:, s_, :],
                                                             op0=ALU.mult, op1=ALU.add),
                     reads=[tpgt, t_colpar, t_f2[s_]], writes=[t_f1[s_]])
                if pend is not None:
                    stage2(pend)
                pend = (c, sub, s_, pup, tpup)
            stage2(pend)
            ring_done(1)
            ck(9)
            dbank = [[bank() for hf in range(2)] for b in range(NB)]
            for pc in range(6):
                ring_fill()
                wdp, twdp = ring_next()
                for cc in range(4):
                    c = pc * 4 + cc
                    if c >= NCH:
                        break
                    for b in range(NB):
                        for hf in range(2):
                            pbk, tpbk = dbank[b][hf]
                            P.op("pe", lambda e: e.matmul(pbk[:], lhsT=aT[:, c, b * 128:(b + 1) * 128],
                                                          rhs=wdp[:, cc * 1024 + hf * 512: cc * 1024 + (hf + 1) * 512],
                                                          start=(c == 0), stop=(c == NCH - 1)),
                                 reads=[t_aT, twdp], writes=[tpbk])
                ring_done(1)
            for b in range(NB):
                for hf in range(2):
                    pbk, tpbk = dbank[b][hf]
                    P.op("dve", lambda e: e.tensor_tensor(out=xres[:, b, hf * 512:(hf + 1) * 512], in0=pbk[:],
                                                          in1=xres[:, b, hf * 512:(hf + 1) * 512], op=ALU.add),
                         reads=[tpbk, t_x[b]], writes=[t_x[b]])

        def main_loop():
            for b in range(NB):
                P.dma("sp", xres[:, b, :], x_d[b * 128:(b + 1) * 128, :], writes=[t_x[b]])
            P.dma("sp", wfm[:], winfm_s[0], reads=[t_sc[("fm", 0)]], writes=[t_wfm])
            P.dma("sp", wtm[:], wintm_s[0], reads=[t_sc[("tm", 0)]], writes=[t_wtm])
            ring_fill()
            for ti in range(ntile):
                for b in range(NB if ti > 0 else 0):
                    P.dma("sp", xres[:, b, :],
                          x_d[ti * TT + b * 128: ti * TT + (b + 1) * 128, :], writes=[t_x[b]])
                for l in range(nl):
                    last = (ti == ntile - 1 and l == nl - 1)
                    rmsnorm_to_hT(l, 0)
                    ck(2)
                    fm_phase(l)
                    if not last:
                        P.dma("sp", wfm[:], winfm_s[(l + 1) % nl], reads=[t_sc[("fm", (l + 1) % nl)]], writes=[t_wfm])
                    ck(3)
                    tm_phase(l)
                    if not last:
                        P.dma("sp", wtm[:], wintm_s[(l + 1) % nl], reads=[t_sc[("tm", (l + 1) % nl)]], writes=[t_wtm])
                    ck(4)
                    P.op("pool", lambda e: e.tensor_tensor(
                        out=sg4[:], in0=sg4[:], in1=rowpar[:, l, 256:512].unsqueeze(1).to_broadcast([128, NB, 256]),
                        op=ALU.mult), reads=[t_sg4, t_rowpar], writes=[t_sg4])
                    rstd_act(vstat[:, 1, :], vstat[:, 0, :], 64, t_vstat, t_vstat)
                    ring_fill()
                    wo = [ring_next(), ring_next()]
                    interleave([chain_c1(l, 0)])
                    for b in range(NB):
                        gb = ti * NB + b
                        gens = [(chain_c2(l, b), 2, 0), chain_b(l, b, gb), (chain_a(l, b), 1, 3)]
                        if b + 1 < NB:
                            gens.append((chain_c1(l, b + 1), 1, 2))
                        if b > 0:
                            gens.append((out_proj(b - 1, wo), 1, 1))
                        interleave(gens)
                        ck(6)
                    interleave([out_proj(NB - 1, wo)])
                    ck(8)
                    ring_done(2)
                    P.op("pool", lambda e: e.tensor_copy(out=kT2[:, l, :, 0:128], in_=kT2[:, l, :, TT:TT + 128]),
                         reads=[t_kT2[l]], writes=[t_kT2[l]])
                    P.op("pool", lambda e: e.tensor_copy(out=vaug[:, l, 0, :, :], in_=vaug[:, l, NB, :, :]),
                         reads=[t_vaug[l]], writes=[t_vaug[l]])
                    rmsnorm_to_hT(l, 8)
                    P.op("pool", lambda e: e.tensor_copy(out=hT[:, :, 0:2], in_=halo[:, l, :, :]),
                         reads=[t_halo[l], t_hT], writes=[t_hT])
                    P.op("pool", lambda e: e.tensor_copy(out=halo[:, l, :, :], in_=hT[:, :, TT:TT + 2]),
                         reads=[t_hT, t_halo[l]], writes=[t_halo[l]])
                    ffn(l)
                t_o = T("out")
                for b in range(NB):
                    P.dma("sp", out_d[ti * TT + b * 128: ti * TT + (b + 1) * 128, :], xres[:, b, :], reads=[t_x[b]],
                          writes=[t_o], add=True)

        try:
            main_loop()
        except _Stop:
            pass
        P.drain("sp")
    build.stats = (P.nops, P.nwait, dict(P.cnt))
    build.marks = P.marks
    return nc


def _t5_bucket(dist):
    max_exact = 16
    n = np.maximum(dist, 0)
    is_small = n < max_exact
    nf = np.maximum(n, 1).astype(np.float32)
    large = max_exact + (np.log(nf / max_exact) / np.log(128 / max_exact) * (32 - max_exact)).astype(np.int32)
    large = np.minimum(large, 31)
    return np.where(is_small, n, large)


def prep_shared(inp, nl=NL):
    f = np.float32
    w_in = np.asarray(inp["w_in"], f)
    tm = list(range(0, 512)) + list(range(1792, 2304)) + list(range(1152, 1280))
    fm = (list(range(512, 1024)) + list(range(1024, 1088)) * 2 + list(range(1088, 1152)) * 2
          + list(range(1280, 1536)) + list(range(1536, 1792)))
    idx = np.array(tm + fm)
    sh = {}
    sh["w_in_r"] = np.ascontiguousarray(w_in[:nl][:, :, idx])
    sh["w_out"] = np.ascontiguousarray(np.asarray(inp["w_out"], f)[:nl])
    sh["w_gate"] = np.ascontiguousarray(np.asarray(inp["w_gate"], f)[:nl])
    sh["w_up"] = np.ascontiguousarray(np.asarray(inp["w_up"], f)[:nl])
    sh["w_down"] = np.ascontiguousarray(np.asarray(inp["w_down"], f)[:nl])
    colpar = np.zeros((128, nl, 110), f)
    lg = np.asarray(inp["hgrn_lb_logits"], f)
    for l in range(nl):
        colpar[:, l, 0:8] = np.asarray(inp["norm1_g"], f)[l].reshape(8, 128).T
        colpar[:, l, 8:16] = np.asarray(inp["norm2_g"], f)[l].reshape(8, 128).T
        colpar[:, l, 16] = np.tile(np.asarray(inp["q_norm_g"], f)[l], 2)
        colpar[:, l, 17] = np.tile(np.asarray(inp["k_norm_g"], f)[l], 2)
        colpar[:, l, 18:20] = lg[0].reshape(2, 128).T
        colpar[:, l, 20:22] = lg[min(1, lg.shape[0] - 1)].reshape(2, 128).T
        cw = np.asarray(inp["conv_w"], f)[l]
        cb = np.asarray(inp["conv_b"], f)[l]
        blk = np.stack([cw[0], cw[1], cw[2], cb], axis=-1).reshape(NCH, 128, 4)
        colpar[:, l, 22:110] = blk.transpose(1, 0, 2).reshape(128, 88)
    sh["colpar"] = colpar
    rowpar = np.zeros((nl, 520), f)
    for l in range(nl):
        rowpar[l, 0:256] = np.asarray(inp["gmlp_vnorm_g"], f)[l].reshape(256)
        rowpar[l, 256:512] = np.tile(np.asarray(inp["hgrn_onorm_g"], f)[l], 4)
        rowpar[l, 512:520] = np.asarray(inp["attn_sinks"], f)[l]
    sh["rowpar"] = rowpar
    ws = np.asarray(inp["gmlp_w_s"], f)[:nl]
    sh["wsT"] = np.ascontiguousarray(ws.transpose(0, 3, 1, 2))
    sh["bT"] = np.ascontiguousarray(np.asarray(inp["gmlp_b_s"], f)[:nl].transpose(0, 2, 1))
    rb = np.asarray(inp["rel_bias"], f)
    j = np.arange(128)[:, None]
    i = np.arange(128)[None, :]
    biasT = np.full((128, 2, 8, 128), -30000.0, f)
    d_prev = i + 128 - j
    d_cur = i - j
    for kb, dd in ((0, d_prev), (1, d_cur)):
        valid = (dd >= 0) & (dd < 128)
        g = rb[_t5_bucket(dd)]
        g = np.where(valid[:, :, None], g, np.float32(-30000.0))
        biasT[:, kb, :, :] = g.transpose(0, 2, 1)
    sh["biasT"] = biasT
    s_ = np.arange(128)[:, None]
    t_ = np.arange(128)[None, :]
    cst = np.zeros((128, 3, 128), f)
    cst[:, 0, :] = (s_ <= t_)
    cst[:, 1, :] = (s_ <= t_) & ((s_ // HCH) == (t_ // HCH))
    cst[:, 2, :] = np.broadcast_to((np.arange(128) % HCH != 0)[None, :], (128, 128))
    sh["cst"] = cst
    sh["cm"] = (np.arange(128)[:, None] // HCH == np.arange(HN)[None, :]).astype(f)
    b2 = np.zeros((128, 2), f)
    b2[0:64, 0] = 1
    b2[64:128, 1] = 1
    sh["blk2"] = b2
    sh["blk2T"] = np.ascontiguousarray(b2.T)
    return sh


_NC_CACHE = {}


def kernel(**inputs):
    x = np.asarray(inputs["x"], np.float32)
    B, S, _ = x.shape
    sh = prep_shared(inputs)
    key = (S, NL)
    if key not in _NC_CACHE:
        _NC_CACHE[key] = build(S, NL)
    nc = _NC_CACHE[key]
    in_maps = []
    for c in range(8):
        m = dict(sh)
        m["x"] = np.ascontiguousarray(x[c % B])
        in_maps.append(m)
    res = run_bass_kernel_spmd(nc, in_maps, core_ids=list(range(8)))
    out = np.stack([np.asarray(res.results[b]["out"], np.float32) for b in range(B)], axis=0)
    return out
```

```python
import contextlib
import numpy as np
import concourse.bass as bass
import concourse.mybir as mybir
from concourse.bass_utils import run_bass_kernel_spmd

F32 = mybir.dt.float32
BF16 = mybir.dt.bfloat16
AF = mybir.ActivationFunctionType
ALU = mybir.AluOpType
AX = mybir.AxisListType

D = 1024
DFF = 2816
NCH = 22
NL = 2
TT = 512
NB = 4
EPS = 1e-6
SOFT_C = 0.0
NDMASEM = 12
NRING = 4
NTM = 1152
HCH = 32
HN = 128 // HCH
NFM = 1280


class T:
    __slots__ = ("name", "w", "r")

    def __init__(self, name):
        self.name = name
        self.w = {}
        self.r = {}


class Prog:
    def __init__(self, nc, stack):
        self.nc = nc
        self.st = stack
        self.eng = {"pe": nc.tensor, "act": nc.scalar, "dve": nc.vector,
                    "pool": nc.gpsimd, "sp": nc.sync}
        self.sem = {e: stack.enter_context(nc.semaphore("s_" + e)) for e in self.eng}
        self.cnt = {e: 0 for e in self.eng}
        self.dsem = {}
        self.dcnt = {}
        self.nds = {"sp": NDMASEM, "pool": 80, "act": 1}
        for q in ("sp", "pool", "act"):
            self.dsem[q] = [stack.enter_context(nc.semaphore("d_%s%d" % (q, i)))
                            for i in range(self.nds[q])]
            self.dcnt[q] = 0
        self.seen = {e: {} for e in self.eng}
        self.nwait = 0
        self.nops = 0
        self.marks = []

    def mark(self, name):
        self.marks.append((name, dict(self.cnt)))

    def sb(self, name, shape, dt=F32):
        return self.st.enter_context(self.nc.sbuf_tensor("sb_" + name, list(shape), dt))

    def _wait(self, e, tok, raw=False):
        kind, key, val = tok
        if kind == "eng":
            if key == e and not (raw and e != "pe"):
                return
            sem = self.sem[key]
            sk = "e_" + key
        else:
            q, i = key
            sem = self.dsem[q][i]
            sk = "d_%s%d" % (q, i)
        if self.seen[e].get(sk, 0) >= val:
            return
        self.eng[e].wait_ge(sem, val)
        self.seen[e][sk] = val
        self.nwait += 1

    def _deps(self, e, reads, writes, add):
        for t in reads:
            for tok in t.w.values():
                self._wait(e, tok, raw=True)
        for t in writes:
            if not add:
                for tok in t.w.values():
                    self._wait(e, tok, raw=True)
            for tok in t.r.values():
                self._wait(e, tok, raw=True)

    def _mark(self, tok, reads, writes, add):
        for t in reads:
            t.r[tok[1]] = tok
        for t in writes:
            if add:
                t.w[tok[1]] = tok
            else:
                t.w = {tok[1]: tok}
            t.r = {}

    def op(self, e, fn, reads=(), writes=()):
        self._deps(e, reads, writes, False)
        ins = fn(self.eng[e])
        self.cnt[e] += 1
        ins.then_inc(self.sem[e], 1)
        self._mark(("eng", e, self.cnt[e]), reads, writes, False)
        self.nops += 1
        return ins

    def dma(self, q, out, in_, reads=(), writes=(), add=False, **kw):
        n = self.dcnt[q]
        slot = n % self.nds[q]
        gen = n // self.nds[q]
        if gen > 0:
            self._wait(q, ("dma", (q, slot), 16 * gen))
        self._deps(q, reads, writes, add)
        ins = self.eng[q].dma_start(out=out, in_=in_, **kw)
        ins.then_inc(self.dsem[q][slot], 16)
        self.dcnt[q] = n + 1
        self._mark(("dma", (q, slot), 16 * (gen + 1)), reads, writes, add)
        return ins

    def drain(self, e, queues=("sp", "pool", "act")):
        for q in queues:
            n = self.dcnt[q]
            for slot in range(self.nds[q]):
                if n > slot:
                    k = (n - 1 - slot) // self.nds[q] + 1
                    self._wait(e, ("dma", (q, slot), 16 * k))


class _Stop(Exception):
    pass


def build(ntok, nl=NL, stop=None, debug=False):
    assert ntok % TT == 0
    ntile = ntok // TT
    nc = bass.Bass("TRN2", target_bir_lowering=False)

    def din(name, shape, dt=F32):
        return nc.dram_tensor(name, list(shape), dt, kind="ExternalInput").ap()

    x_d = din("x", [ntok, D])
    win_d = din("w_in_r", [nl, D, NTM + NFM])
    wout_d = din("w_out", [nl, D, D])
    wg_d = din("w_gate", [nl, D, DFF])
    wu_d = din("w_up", [nl, D, DFF])
    wd_d = din("w_down", [nl, DFF, D])
    colpar_d = din("colpar", [128, nl, 110])
    rowpar_d = din("rowpar", [nl, 520])
    wsT_d = din("wsT", [nl, 128, 4, 128])
    bT_d = din("bT", [nl, 128, 4])
    biasT_d = din("biasT", [128, 2, 8, 128])
    cst_d = din("cst", [128, 3, 128])
    cm_d = din("cm", [128, HN])
    blk2_d = din("blk2", [128, 2])
    blk2T_d = din("blk2T", [2, 128])
    out_d = nc.dram_tensor("out", [ntok, D], F32, kind="ExternalOutput").ap()

    def dscr(name, shape):
        return nc.dram_tensor(name, list(shape), BF16, kind="Internal").ap()

    winfm_s = dscr("winfm_s", [nl, 128, 8, NFM])
    wintm_s = dscr("wintm_s", [nl, 128, 8, NTM])
    wout_s = dscr("wout_s", [nl, 2, 128, 8, 512])
    wg_s = dscr("wg_s", [nl, 11, 128, 8, 256])
    wu_s = dscr("wu_s", [nl, 11, 128, 8, 256])
    wd_s = dscr("wd_s", [nl, 6, 128, 4, 1024])

    with contextlib.ExitStack() as st:
        P = Prog(nc, st)
        sb = P.sb

        ident = sb("ident", [128, 128], BF16); t_ident = T("ident")
        cst = sb("cst", [128, 3, 128]); t_cst = T("cst")
        cm = sb("cm", [128, HN]); t_cm = T("cm")
        blk2 = sb("blk2", [128, 2]); blk2T = sb("blk2T", [2, 128]); t_blk = T("blk")
        colpar = sb("colpar", [128, nl, 110]); t_colpar = T("colpar")
        rowpar = sb("rowpar", [128, nl, 520]); t_rowpar = T("rowpar")
        esink = sb("esink", [128, nl, 8]); t_esink = T("esink")
        wsT = sb("wsT", [128, nl, 4, 128], BF16); t_wsT = T("wsT")
        bT = sb("bT", [128, nl, 4]); t_bT = T("bT")
        bias8 = sb("bias8", [128, 2, 2, 2, 2, 128], BF16); t_bias8 = T("bias8")
        lbp = sb("lbp", [128, nl, 2, 2]); t_lbp = T("lbp")

        cqf = sb("cqf", [128, 4, TT]); t_cqf = T("cqf")
        stage = cqf[:].rearrange("p a t -> p (a t)").rearrange("p (a b c) -> p a b c", a=2, b=8)
        t_stage = t_cqf
        P.op("pool", lambda e: e.memset(ident[:], 0.0), writes=[t_ident])
        P.op("pool", lambda e: e.affine_select(out=ident[:], in_=ident[:], pattern=[[-1, 128]],
                                               compare_op=ALU.not_equal, fill=1.0, base=0,
                                               channel_multiplier=1),
             reads=[t_ident], writes=[t_ident])
        P.dma("sp", cst[:], cst_d, writes=[t_cst])
        P.dma("sp", cm[:], cm_d, writes=[t_cm])
        P.dma("sp", blk2[:], blk2_d, writes=[t_blk])
        P.dma("sp", blk2T[:], blk2T_d, writes=[t_blk], add=True)
        P.dma("sp", colpar[:], colpar_d, writes=[t_colpar])
        for l in range(nl):
            P.dma("sp", rowpar[:, l, :], rowpar_d[l].partition_broadcast(128),
                  writes=[t_rowpar], add=(l > 0))
        P.dma("sp", bT[:], bT_d.rearrange("l p g -> p l g"), writes=[t_bT])
        for l in range(nl):
            P.dma("sp", stage[:, 0, 0:4, :], wsT_d[l], writes=[t_stage])
            P.op("dve", lambda e: e.tensor_tensor(
                out=wsT[:, l, :, :], in0=stage[:, 0, 0:4, :],
                in1=cst[:, 0, :].unsqueeze(1).to_broadcast([128, 4, 128]), op=ALU.mult),
                reads=[t_stage, t_cst], writes=[t_wsT])
        P.dma("sp", stage, biasT_d, writes=[t_stage])
        for g in range(2):
            for e2 in range(2):
                P.op("dve", lambda e: e.tensor_scalar(out=bias8[:, g, e2, :, :, :],
                                                      in0=stage[:, :, 4 * g + e2:4 * g + 4:2, :],
                                                      scalar1=8.0, scalar2=None, op0=ALU.mult),
                     reads=[t_stage], writes=[t_bias8])
        for l in range(nl):
            P.op("act", lambda e: e.activation(out=esink[:, l, :], in_=rowpar[:, l, 512:520],
                                               func=AF.Exp),
                 reads=[t_rowpar], writes=[t_esink])
        P.op("dve", lambda e: e.tensor_scalar(out=esink[:], in0=esink[:], scalar1=float(np.exp(-SOFT_C)),
                                              scalar2=None, op0=ALU.mult),
             reads=[t_esink], writes=[t_esink])
        ltmp = sb("ltmp", [128, 8]); t_ltmp = T("ltmp")
        if nl == 2:
            P.op("dve", lambda e: e.tensor_tensor(out=ltmp[:, 0:2], in0=colpar[:, 0, 20:22],
                                                  in1=colpar[:, 0, 18:20], op=ALU.subtract),
                 reads=[t_colpar], writes=[t_ltmp])
            P.op("act", lambda e: e.activation(out=ltmp[:, 2:4], in_=ltmp[:, 0:2], func=AF.Sigmoid),
                 reads=[t_ltmp], writes=[t_ltmp])
        P.op("dve", lambda e: e.memset(lbp[:], 0.0), writes=[t_lbp])
        for pr in range(2):
            P.op("dve", lambda e: e.memset(lbp[:, 0, pr, 0:1], 1.0), reads=[t_lbp], writes=[t_lbp])
            if nl == 2:
                P.op("dve", lambda e: e.tensor_copy(out=lbp[:, 1, pr, 1:2], in_=ltmp[:, 2 + pr:3 + pr]),
                     reads=[t_ltmp, t_lbp], writes=[t_lbp])
                P.op("dve", lambda e: e.tensor_scalar(out=lbp[:, 1, pr, 0:1], in0=ltmp[:, 2 + pr:3 + pr],
                                                      scalar1=-1.0, scalar2=1.0, op0=ALU.mult, op1=ALU.add),
                     reads=[t_ltmp, t_lbp], writes=[t_lbp])

        t_sc = {}
        for l in range(nl):
            for nm in ("fm", "tm", "wo", "gu", "wd"):
                t_sc[(nm, l)] = T("sc_%s%d" % (nm, l))

        def cast(dst, src, t):
            P.dma("pool", dst, src, writes=[t], add=True)

        def emit_casts():
            for l in range(nl):
                for dc in range(8):
                    rows = slice(dc * 128, (dc + 1) * 128)
                    cast(winfm_s[l, :, dc, :], win_d[l, rows, NTM:NTM + NFM], t_sc[("fm", l)])
                for dc in range(8):
                    rows = slice(dc * 128, (dc + 1) * 128)
                    cast(wintm_s[l, :, dc, :], win_d[l, rows, 0:NTM], t_sc[("tm", l)])
                for dc in range(8):
                    rows = slice(dc * 128, (dc + 1) * 128)
                    cast(wout_s[l, :, :, dc, :].rearrange("h p c -> p h c"),
                         wout_d[l, rows, :].rearrange("p (h c) -> p h c", c=512), t_sc[("wo", l)])
                for dc in range(8):
                    rows = slice(dc * 128, (dc + 1) * 128)
                    cast(wg_s[l, :, :, dc, :].rearrange("g p c -> p g c"),
                         wg_d[l, rows, :].rearrange("p (g c) -> p g c", c=256), t_sc[("gu", l)])
                    cast(wu_s[l, :, :, dc, :].rearrange("g p c -> p g c"),
                         wu_d[l, rows, :].rearrange("p (g c) -> p g c", c=256), t_sc[("gu", l)])
                for pc in range(6):
                    ncc = 4 if pc < 5 else 2
                    cast(wd_s[l, pc, :, 0:ncc, :],
                         wd_d[l, pc * 512:pc * 512 + ncc * 128, :].rearrange("(cc p) d -> p cc d", p=128),
                         t_sc[("wd", l)])

        xres = sb("xres", [128, NB, D]); t_x = [T("x%d" % b) for b in range(NB)]
        hT = sb("hT", [128, 8, 2 + TT], BF16); t_hT = T("hT")
        halo = sb("halo", [128, nl, 8, 2], BF16); t_halo = [T("halo%d" % l) for l in range(nl)]
        wfm = sb("wfm", [128, 8, NFM], BF16); t_wfm = T("wfm")
        wtm = sb("wtm", [128, 8, NTM], BF16); t_wtm = T("wtm")
        ring = sb("ring", [128, NRING, 4096], BF16); t_ring = [T("ring%d" % i) for i in range(NRING)]
        aT = sb("aT", [128, NCH, TT], BF16); t_aT = T("aT")
        qT = sb("qT", [128, 4, TT], BF16); t_qT = T("qT")
        kT2 = sb("kT2", [128, nl, 2, 128 + TT], BF16); t_kT2 = [T("kT2%d" % l) for l in range(nl)]
        vaug = sb("vaug", [128, nl, NB + 1, 2, 65], BF16); t_vaug = [T("vaug%d" % l) for l in range(nl)]
        Sst = sb("Sst", [128, nl, 2, HN + 1, 64]); t_S = [T("S%d" % l) for l in range(nl)]
        sqb = sb("sqb", [128, 2, TT]); t_sqb = [T("sqb0"), T("sqb1")]
        epsc = sb("epsc", [128, 1]); t_epsc = T("epsc")
        nstat = sb("nstat", [128, 8]); t_nstat = T("nstat")
        hs = sb("hs", [128, 2, D], BF16); t_hs = [T("hs0"), T("hs1")]
        u4 = sb("u4", [128, NB, 256], BF16); t_u4 = T("u4")
        vv4 = sb("vv4", [128, NB, 256], BF16); t_vv4 = T("vv4")
        vh4 = sb("vh4", [128, NB, 256], BF16); t_vh4 = T("vh4")
        sg4 = sb("sg4", [128, NB, 256], BF16); t_sg4 = T("sg4")
        vtmp = sb("vtmp", [128, 2, 256]); t_vtmp = [T("vtmp0"), T("vtmp1")]
        vsq = sb("vsq", [128, 2, 256]); t_vsq = [T("vsq0"), T("vsq1")]
        vstat = sb("vstat", [128, 2, NB * 4]); t_vstat = T("vstat")
        rst2v = [vtmp[:].rearrange("p a c -> p (a c)")[0:2, :], vsq[:].rearrange("p a c -> p (a c)")[0:2, :]]
        t_rst2 = [t_vtmp, t_vsq]
        vn = sb("vn", [128, 256], BF16); t_vn = T("vn")
        vexp = sb("vexp", [128, 4, HN, 64], BF16); t_vexp = T("vexp")
        mixed2 = sb("mixed", [128, 2, D], BF16); t_mixed2 = [T("mixed0"), T("mixed1")]
        mixT = sb("mixT", [128, 8, 128], BF16); t_mixT = T("mixT")
        junk = mixT[:].rearrange("p c t -> p (c t)"); t_junk = t_mixT
        PT = sb("PT", [128, 2, 4, 128], BF16); t_PT = [T("PT0"), T("PT1")]
        hg = {}
        for nm in ("A", "B", "C", "E"):
            hg[nm] = (sb("hg_" + nm, [128, 2, 128]), T("hg_" + nm))
        for nm in ("qd", "kd", "kl"):
            hg[nm] = (sb("hg_" + nm, [128, 2, 2, 128], BF16), [T("hg_" + nm + "0"), T("hg_" + nm + "1")])
        dec2 = sb("dec", [128, 2, 2, HN]); t_dec2 = [T("dec0"), T("dec1")]
        kltok = sb("kltok", [128, 2, 2, 128], BF16); t_kltok = T("kltok")
        attnT = sb("attnT", [128, 4, 128], BF16); t_attnT = T("attnT")
        Qm = sb("Qm", [128, 2, HN, 128], BF16); t_Qm = T("Qm")
        Sbf = sb("Sbf", [128, 2, HN, 64], BF16); t_Sbf = T("Sbf")
        osq_t = vsq; t_osq = t_vsq[1]
        yc = sb("yc", [128, 256]); t_yc = T("yc")
        ostat = sb("ostat", [128, 8]); t_ostat = T("ostat")
        den = sb("den", [128, 2, 8]); t_den = [T("den0"), T("den1")]
        f1 = sb("f1", [128, 2, 256]); t_f1 = [T("f1a"), T("f1b")]
        f2 = sb("f2", [128, 2, 256]); t_f2 = [T("f2a"), T("f2b")]
        pbs = [st.enter_context(nc.psum_tensor("pb%d" % i, [128, 512], F32)) for i in range(8)]
        t_pb = [T("pb%d" % i) for i in range(8)]
        bank_i = [0]

        def bank():
            i = bank_i[0] % 8
            bank_i[0] += 1
            return pbs[i], t_pb[i]

        P.op("pool", lambda e: e.memset(epsc[:], EPS), writes=[t_epsc])
        P.op("pool", lambda e: e.memset(Qm[:], 0.0), writes=[t_Qm])
        P.op("pool", lambda e: e.memset(kltok[:], 0.0), writes=[t_kltok])
        P.op("pool", lambda e: e.memset(Sst[:], 0.0), writes=t_S)
        P.op("pool", lambda e: e.memset(halo[:], 0.0), writes=t_halo)
        P.op("pool", lambda e: e.memset(kT2[:], 0.0), writes=t_kT2)
        P.op("pool", lambda e: e.memset(vaug[:], 0.0), writes=t_vaug)
        for l in range(nl):
            P.op("pool", lambda e: e.memset(vaug[:, l, :, :, 64:65], 1.0), reads=[t_vaug[l]], writes=[t_vaug[l]])

        emit_casts()

        items = []
        for ti in range(ntile):
            for l in range(nl):
                for hf in range(2):
                    items.append([(wout_s[l, hf].rearrange("p a b -> p (a b)"), 0, 4096, t_sc[("wo", l)])])
                for g in range(11):
                    items.append([(wg_s[l, g].rearrange("p a b -> p (a b)"), 0, 2048, t_sc[("gu", l)]),
                                  (wu_s[l, g].rearrange("p a b -> p (a b)"), 2048, 2048, t_sc[("gu", l)])])
                for pc in range(6):
                    nv = 4096 if pc < 5 else 2048
                    items.append([(wd_s[l, pc].rearrange("p a b -> p (a b)")[:, 0:nv], 0, nv, t_sc[("wd", l)])])
        rstate = {"issued": 0, "consumed": 0, "released": 0}

        def ring_fill():
            while rstate["issued"] < len(items) and rstate["issued"] < rstate["released"] + NRING:
                j = rstate["issued"]
                s = j % NRING
                for k, (src, off, n, tsrc) in enumerate(items[j]):
                    P.dma("sp", ring[:, s, off:off + n], src, reads=[tsrc], writes=[t_ring[s]], add=(k > 0))
                rstate["issued"] += 1

        def ring_done(k=1):
            rstate["released"] += k
            ring_fill()

        def ring_next():
            j = rstate["consumed"]
            assert j < rstate["issued"]
            rstate["consumed"] += 1
            s = j % NRING
            return ring[:, s, :], t_ring[s]

        cp = lambda l, a, b: colpar[:, l, a:b]

        dbgs = {}

        def dbg(name, ap, t, shape):
            if not debug or name in dbgs:
                return
            d = nc.dram_tensor("dbg_" + name, list(shape), ap.dtype, kind="ExternalOutput").ap()
            P.dma("sp", d, ap, reads=[t], writes=[T("dbgo_" + name)])
            dbgs[name] = True

        def ck(k):
            P.mark("ck%s" % k)
            if stop is not None and abs(stop - k) < 1e-6:
                raise _Stop()

        def rstd_act(out, in_, n, t_in, t_out):
            t_in = t_in if isinstance(t_in, list) else [t_in]
            t_out = t_out if isinstance(t_out, list) else [t_out]
            P.op("act", lambda e: e.activation(out=out, in_=in_, func=AF.Ln, scale=1.0 / n,
                                               bias=epsc[0:out.shape[0], :]),
                 reads=t_in + [t_epsc], writes=t_out)
            P.op("act", lambda e: e.activation(out=out, in_=out, func=AF.Exp, scale=-0.5),
                 reads=t_out, writes=t_out)

        def rmsnorm_to_hT(l, gcol0):
            for b in range(NB):
                P.op("act", lambda e: e.activation(out=junk, in_=xres[:, b, :], func=AF.Square,
                                                   accum_out=nstat[:, b:b + 1]),
                     reads=[t_x[b]], writes=[t_junk, t_nstat])
            rstd_act(nstat[:, 4:8], nstat[:, 0:4], D, t_nstat, t_nstat)
            for b in range(NB):
                s_ = b % 2
                P.op("act", lambda e: e.activation(out=hs[:, s_, :], in_=xres[:, b, :], func=AF.Identity,
                                                   scale=nstat[:, 4 + b:5 + b]),
                     reads=[t_x[b], t_nstat], writes=[t_hs[s_]])
                pb, tpb = bank()
                pbb = pb[:].bitcast(BF16)
                for dc in range(8):
                    P.op("pe", lambda e: e.transpose(out=pbb[:, dc * 128:(dc + 1) * 128],
                                                     in_=hs[:, s_, dc * 128:(dc + 1) * 128], identity=ident[:]),
                         reads=[t_hs[s_], t_ident], writes=[tpb])
                P.op("dve", lambda e: e.tensor_tensor(
                    out=hT[:, :, 2 + b * 128:2 + (b + 1) * 128],
                    in0=pbb.rearrange("p (c t) -> p c t", t=128),
                    in1=cp(l, gcol0, gcol0 + 8).unsqueeze(2).to_broadcast([128, 8, 128]), op=ALU.mult),
                    reads=[tpb, t_colpar], writes=[t_hT])

        hTv = hT[:, :, 2:2 + TT]

        def fm_phase(l):
            def main_mm(ft):
                pb, tpb = bank()
                for dc in range(8):
                    P.op("pe", lambda e: e.matmul(pb[:], lhsT=wfm[:, dc, ft * 128:(ft + 1) * 128],
                                                  rhs=hTv[:, dc, :], start=(dc == 0), stop=(dc == 7)),
                         reads=[t_wfm, t_hT], writes=[tpb])
                return pb, tpb

            def dst_of(ft):
                if ft < 4:
                    return qT[:, ft, :], t_qT, 16
                return kT2[:, l, ft - 4, 128:128 + TT], t_kT2[l], 17

            def stage_a(ft):
                pb, tpb = main_mm(ft)
                dst, tdst, _ = dst_of(ft)
                s_ = ft % 2
                P.op("act", lambda e: e.activation(out=dst, in_=pb[:], func=AF.Copy), reads=[tpb], writes=[tdst])
                P.op("act", lambda e: e.activation(out=sqb[:, s_, :], in_=pb[:], func=AF.Square),
                     reads=[tpb], writes=[t_sqb[s_]])

            def stage_b(ft):
                s_ = ft % 2
                pb2, tpb2 = bank()
                P.op("pe", lambda e: e.matmul(pb2[0:2, :], lhsT=blk2[:], rhs=sqb[:, s_, :], start=True, stop=True),
                     reads=[t_blk, t_sqb[s_]], writes=[tpb2])
                rstd_act(rst2v[s_], pb2[0:2, :], 64, tpb2, t_rst2[s_])

            def stage_c(ft):
                s_ = ft % 2
                dst, tdst, gc = dst_of(ft)
                pb3, tpb3 = bank()
                P.op("pe", lambda e: e.matmul(pb3[:], lhsT=blk2T[:], rhs=rst2v[s_], start=True, stop=True),
                     reads=[t_blk] + t_rst2[s_], writes=[tpb3])
                P.op("dve", lambda e: e.scalar_tensor_tensor(out=dst, in0=pb3[:], scalar=cp(l, gc, gc + 1),
                                                             in1=dst, op0=ALU.mult, op1=ALU.mult),
                     reads=[tpb3, tdst, t_colpar], writes=[tdst])

            for step in range(8):
                if step < 6:
                    stage_a(step)
                if 0 <= step - 1 < 6:
                    stage_b(step - 1)
                if 0 <= step - 2 < 6:
                    stage_c(step - 2)
                if step == 5:
                    for ft in (8, 9):
                        pb, tpb = main_mm(ft)
                        P.op("act", lambda e: e.activation(out=cqf[:, ft - 6, :], in_=pb[:], func=AF.Copy),
                             reads=[tpb], writes=[t_cqf])
            for ft in (6, 7):
                pb, tpb = main_mm(ft)
                P.op("act", lambda e: e.activation(out=cqf[:, ft - 6, :], in_=pb[:], func=AF.Silu),
                     reads=[tpb], writes=[t_cqf])

        def tm_phase(l):
            def grp(b, c0, n):
                pb, tpb = bank()
                cols = slice(b * 128, (b + 1) * 128)
                for dc in range(8):
                    P.op("pe", lambda e: e.matmul(pb[:, 0:n], lhsT=hTv[:, dc, cols], rhs=wtm[:, dc, c0:c0 + n],
                                                  start=(dc == 0), stop=(dc == 7)),
                         reads=[t_hT, t_wtm], writes=[tpb])
                return pb, tpb
            for b in range(NB):
                p2, tp2 = grp(b, 512, 512)
                P.op("act", lambda e: e.activation(out=sg4[:, b, :], in_=p2[:, 256:512], func=AF.Silu),
                     reads=[tp2], writes=[t_sg4])
                P.op("act", lambda e: e.activation(out=vh4[:, b, :], in_=p2[:, 0:256], func=AF.Copy),
                     reads=[tp2], writes=[t_vh4])
            for b in range(NB):
                p3, tp3 = grp(b, 1024, 128)
                P.op("act", lambda e: e.activation(
                    out=vaug[:, l, b + 1, :, 0:64], in_=p3[:, 0:128].rearrange("p (g d) -> p g d", d=64),
                    func=AF.Copy), reads=[tp3], writes=[t_vaug[l]])
            for b in range(NB):
                p1, tp1 = grp(b, 0, 512)
                s_ = b % 2
                P.op("act", lambda e: e.activation(out=u4[:, b, :], in_=p1[:, 0:256], func=AF.Gelu),
                     reads=[tp1], writes=[t_u4])
                P.op("act", lambda e: e.activation(out=vtmp[:, s_, :], in_=p1[:, 256:512], func=AF.Gelu),
                     reads=[tp1], writes=[t_vtmp[s_]])
                P.op("pool", lambda e: e.tensor_tensor(out=vsq[:, s_, :], in0=vtmp[:, s_, :], in1=vtmp[:, s_, :], op=ALU.mult),
                     reads=[t_vtmp[s_]], writes=[t_vsq[s_]])
                P.op("dve", lambda e: e.tensor_reduce(out=vstat[:, 0, b * 4:(b + 1) * 4],
                                                      in_=vsq[:, s_, :].rearrange("p (g d) -> p g d", d=64),
                                                      axis=AX.X, op=ALU.add),
                     reads=[t_vsq[s_]], writes=[t_vstat])
                P.op("pool", lambda e: e.tensor_copy(out=vv4[:, b, :], in_=vtmp[:, s_, :]),
                     reads=[t_vtmp[s_]], writes=[t_vv4])

        def chain_a(l, b):
            mixed = mixed2[:, b % 2, :]; t_mixed = t_mixed2[b % 2]
            for g in range(4):
                P.op("dve", lambda e: e.scalar_tensor_tensor(
                    out=vn[:, g * 64:(g + 1) * 64], in0=vv4[:, b, g * 64:(g + 1) * 64],
                    scalar=vstat[:, 1, b * 4 + g:b * 4 + g + 1], in1=rowpar[:, l, g * 64:(g + 1) * 64],
                    op0=ALU.mult, op1=ALU.mult),
                    reads=[t_vv4, t_vstat, t_rowpar], writes=[t_vn])
            yield
            pa, tpa = bank()
            for g in range(4):
                P.op("pe", lambda e: e.matmul(pa[:, g * 64:(g + 1) * 64], lhsT=wsT[:, l, g, :],
                                              rhs=vn[:, g * 64:(g + 1) * 64], start=True, stop=True),
                     reads=[t_wsT, t_vn], writes=[tpa])
            yield
            for g in range(4):
                P.op("dve", lambda e: e.scalar_tensor_tensor(
                    out=mixed[:, g * 64:(g + 1) * 64], in0=pa[:, g * 64:(g + 1) * 64],
                    scalar=bT[:, l, g:g + 1], in1=u4[:, b, g * 64:(g + 1) * 64], op0=ALU.add, op1=ALU.mult),
                    reads=[tpa, t_bT, t_u4], writes=[t_mixed])
                if g % 2 == 1:
                    yield

        def chain_b(l, b, gb):
            mixed = mixed2[:, b % 2, :]; t_mixed = t_mixed2[b % 2]
            cols = slice(b * 128, (b + 1) * 128)
            kbs = [1] if gb == 0 else [0, 1]
            for g in range(2):
                for e2 in range(2):
                    ps_, tps = bank()
                    P.op("pe", lambda e: e.matmul(ps_[:], lhsT=ident[:],
                                                  rhs=bias8[:, g, e2, :, :, :].rearrange("p k r i -> p (k r i)"),
                                                  start=True, stop=False),
                         reads=[t_ident, t_bias8], writes=[tps])
                    pr_ = slice(64 * e2, 64 * e2 + 64)
                    nmm = len(kbs) * 2
                    imm = 0
                    for kb in kbs:
                        for rp in range(2):
                            h = 4 * g + 2 * rp + e2
                            m = h // 2
                            kc = slice(b * 128 + kb * 128, b * 128 + kb * 128 + 128)
                            imm += 1
                            P.op("pe", lambda e: e.matmul(ps_[:, (kb * 2 + rp) * 128:(kb * 2 + rp + 1) * 128],
                                                          lhsT=kT2[pr_, l, g, kc], rhs=qT[pr_, m, cols],
                                                          start=False, stop=(imm == nmm)),
                                 reads=[t_kT2[l], t_qT], writes=[tps])
                    yield
                    P.op("act", lambda e: e.activation(out=PT[:, :, e2:4:2, :],
                                                       in_=ps_[:].rearrange("p (k r i) -> p k r i", k=2, r=2),
                                                       func=AF.Exp, scale=0.125),
                         reads=[tps], writes=[t_PT[0], t_PT[1]])
                    yield
                po, tpo = bank()
                for r in range(4):
                    for kb in kbs:
                        P.op("pe", lambda e: e.matmul(po[:, r * 65:(r + 1) * 65], lhsT=PT[:, kb, r, :],
                                                      rhs=vaug[:, l, b + kb, g, :], start=(kb == kbs[0]),
                                                      stop=(kb == 1)),
                             reads=[t_PT[kb], t_vaug[l]], writes=[tpo])
                yield
                pov = po[:, 0:260].rearrange("p (r d) -> p r d", d=65)
                P.op("dve", lambda e: e.tensor_tensor(out=den[:, g, 0:4].unsqueeze(2), in0=pov[:, :, 64:65],
                                                      in1=esink[:, l, 4 * g:4 * g + 4].unsqueeze(2), op=ALU.add),
                     reads=[tpo, t_esink], writes=[t_den[g]])
                yield
                P.op("dve", lambda e: e.reciprocal(out=den[:, g, 4:8], in_=den[:, g, 0:4]),
                     reads=[t_den[g]], writes=[t_den[g]])
                yield
                P.op("dve", lambda e: e.tensor_tensor(
                    out=mixed[:, 256 + g * 256:256 + (g + 1) * 256].rearrange("p (r d) -> p r d", d=64),
                    in0=pov[:, :, 0:64], in1=den[:, g, 4:8].unsqueeze(2).to_broadcast([128, 4, 64]), op=ALU.mult),
                    reads=[tpo, t_den[g]], writes=[t_mixed])
                yield

        def chain_c1(l, b):
            cols = slice(b * 128, (b + 1) * 128)
            pp = b % 2
            A, tA = hg["A"]; Bq, tB = hg["B"]; C, tC = hg["C"]; E, tE = hg["E"]
            qd, tqd = hg["qd"][0][:, pp], hg["qd"][1][pp]
            kd, tkd = hg["kd"][0][:, pp], hg["kd"][1][pp]
            kl, tkl = hg["kl"][0][:, pp], hg["kl"][1][pp]
            dec, t_dec = dec2[:, pp], t_dec2[pp]
            q_ = cqf[:, 0:2, cols]
            zf = cqf[:, 2:4, cols]
            P.op("act", lambda e: e.activation(out=A[:], in_=zf, func=AF.Exp, scale=-1.0), reads=[t_cqf], writes=[tA])
            yield
            P.op("act", lambda e: e.activation(out=A[:], in_=A[:], func=AF.Ln, bias=1.0), reads=[tA], writes=[tA])
            yield
            P.op("act", lambda e: e.activation(out=A[:], in_=A[:], func=AF.Exp, scale=-1.0), reads=[tA], writes=[tA])
            yield
            for pr in range(2):
                P.op("dve", lambda e: e.tensor_scalar(out=Bq[:, pr, :], in0=A[:, pr, :],
                                                      scalar1=lbp[:, l, pr, 0:1], scalar2=lbp[:, l, pr, 1:2],
                                                      op0=ALU.mult, op1=ALU.add),
                     reads=[tA, t_lbp], writes=[tB])
            yield
            P.op("act", lambda e: e.activation(out=A[:], in_=Bq[:], func=AF.Ln), reads=[tB], writes=[tA])
            yield
            for pr in range(2):
                P.op("dve", lambda e: e.tensor_tensor_scan(out=C[:, pr, :], data0=cst[:, 2, :],
                                                           data1=A[:, pr, :], initial=0.0,
                                                           op0=ALU.mult, op1=ALU.add),
                     reads=[tA, t_cst], writes=[tC])
            yield
            P.op("pool", lambda e: e.tensor_scalar(out=Bq[:], in0=Bq[:], scalar1=-1.0, scalar2=1.0,
                                                   op0=ALU.mult, op1=ALU.add), reads=[tB], writes=[tB])
            yield
            P.op("act", lambda e: e.activation(out=A[:], in_=C[:], func=AF.Exp), reads=[tC], writes=[tA])
            P.op("act", lambda e: e.activation(out=E[:], in_=C[:], func=AF.Exp, scale=-1.0), reads=[tC], writes=[tE])
            cum4 = C[:].rearrange("p a (n j) -> p (a n) j", j=HCH)
            P.op("act", lambda e: e.activation(out=dec.rearrange("p a n -> p (a n)").unsqueeze(2),
                                               in_=cum4[:, :, HCH - 1:HCH], func=AF.Exp),
                 reads=[tC], writes=[t_dec])
            yield
            P.op("dve", lambda e: e.tensor_tensor(out=qd, in0=q_, in1=A[:], op=ALU.mult),
                 reads=[t_cqf, tA], writes=[tqd])
            yield
            P.op("pool", lambda e: e.tensor_tensor(out=kd, in0=Bq[:], in1=E[:], op=ALU.mult),
                 reads=[tB, tE], writes=[tkd])
            yield
            P.op("dve", lambda e: e.tensor_tensor(
                out=A[:].rearrange("p a (n j) -> p (a n) j", j=HCH),
                in0=cum4[:, :, HCH - 1:HCH].to_broadcast([128, 2 * HN, HCH]), in1=cum4, op=ALU.subtract),
                reads=[tC, tA], writes=[tA])
            yield
            P.op("act", lambda e: e.activation(out=A[:], in_=A[:], func=AF.Exp), reads=[tA], writes=[tA])
            yield
            P.op("pool", lambda e: e.tensor_tensor(out=kl, in0=Bq[:], in1=A[:], op=ALU.mult),
                 reads=[tB, tA], writes=[tkl])
            yield

        def emit_vexp(b):
            for h in range(4):
                P.op("pool", lambda e: e.tensor_tensor(
                    out=vexp[:, h, :, :], in0=vh4[:, b, h * 64:(h + 1) * 64].unsqueeze(1).to_broadcast([128, HN, 64]),
                    in1=cm[:].unsqueeze(2).to_broadcast([128, HN, 64]), op=ALU.mult),
                    reads=[t_vh4, t_cm], writes=[t_vexp])

        def chain_c2(l, b):
            pp = b % 2
            mixed = mixed2[:, pp, :]; t_mixed = t_mixed2[pp]
            qd, tqd = hg["qd"][0][:, pp], hg["qd"][1][pp]
            kd, tkd = hg["kd"][0][:, pp], hg["kd"][1][pp]
            kl, tkl = hg["kl"][0][:, pp], hg["kl"][1][pp]
            dec, t_dec = dec2[:, pp], t_dec2[pp]
            if b == 0:
                emit_vexp(0)
                yield
            for pr in range(2):
                v = Qm[:, pr, :, :]
                dst = bass.AP(v.tensor, v.offset, [list(v.ap[0]), [128 + HCH, HN], [1, HCH]])
                P.op("pool", lambda e: e.tensor_copy(out=dst, in_=qd[:, pr, :].rearrange("p (n j) -> p n j", j=HCH)),
                     reads=[tqd], writes=[t_Qm])
            for e2 in range(2):
                pat, tpat = bank()
                rows = slice(64 * e2, 64 * e2 + 64)
                for pr in range(2):
                    P.op("pe", lambda e: e.matmul(pat[:, pr * 128:(pr + 1) * 128], lhsT=kd[rows, pr, :],
                                                  rhs=qd[rows, pr, :], start=True, stop=True),
                         reads=[tkd, tqd], writes=[tpat])
                yield
                P.op("dve", lambda e: e.tensor_tensor(
                    out=attnT[:, e2:4:2, :], in0=pat[:, 0:256].rearrange("p (h t) -> p h t", t=128),
                    in1=cst[:, 1, :].unsqueeze(1).to_broadcast([128, 2, 128]), op=ALU.mult),
                    reads=[tpat, t_cst], writes=[t_attnT])
                yield
            pk, tpk = bank()
            pkb = pk[:].bitcast(BF16)
            for pr in range(2):
                P.op("pe", lambda e: e.transpose(out=pkb[:, pr * 128:(pr + 1) * 128], in_=kl[:, pr, :],
                                                 identity=ident[:]),
                     reads=[tkl, t_ident], writes=[tpk])
            yield
            for pr in range(2):
                v = kltok[:, pr, :, :]
                dstk = bass.AP(v.tensor, v.offset, [list(v.ap[0]), [192, 2], [1, 64]])
                P.op("act", lambda e: e.activation(out=dstk, in_=pkb[:, pr * 128:(pr + 1) * 128].rearrange("p (a k) -> p a k", k=64),
                                                   func=AF.Copy),
                     reads=[tpk], writes=[t_kltok])
            yield
            pds = []
            for pr in range(2):
                pd_, tpd = bank()
                for e2 in range(2):
                    h = 2 * pr + e2
                    P.op("pe", lambda e: e.matmul(pd_[:, 0:HN * 64], lhsT=kltok[:, pr, e2, :],
                                                  rhs=vexp[:, h, :, :].rearrange("p n v -> p (n v)"),
                                                  start=(e2 == 0), stop=(e2 == 1)),
                         reads=[t_kltok, t_vexp], writes=[tpd])
                pds.append((pd_, tpd))
                yield
            if b + 1 < NB:
                emit_vexp(b + 1)
            for n in range(HN):
                for pr in range(2):
                    pd_, tpd = pds[pr]
                    P.op("dve", lambda e: e.scalar_tensor_tensor(
                        out=Sst[:, l, pr, n + 1, :], in0=Sst[:, l, pr, n, :], scalar=dec[:, pr, n:n + 1],
                        in1=pd_[:, n * 64:(n + 1) * 64], op0=ALU.mult, op1=ALU.add),
                        reads=[t_S[l], t_dec, tpd], writes=[t_S[l]])
                yield
            P.op("act", lambda e: e.activation(out=Sbf[:], in_=Sst[:, l, :, 0:HN, :], func=AF.Copy),
                 reads=[t_S[l]], writes=[t_Sbf])
            P.op("pool", lambda e: e.tensor_copy(out=Sst[:, l, :, 0, :], in_=Sst[:, l, :, HN, :]),
                 reads=[t_S[l]], writes=[t_S[l]])
            yield
            pqs = []
            for e2 in range(2):
                pq, tpq = bank()
                rows = slice(64 * e2, 64 * e2 + 64)
                for pr in range(2):
                    h = 2 * pr + e2
                    P.op("pe", lambda e: e.matmul(pq[:, pr * 64:(pr + 1) * 64], lhsT=attnT[:, h, :],
                                                  rhs=vh4[:, b, h * 64:(h + 1) * 64], start=True, stop=False),
                         reads=[t_attnT, t_vh4], writes=[tpq])
                    for n in range(HN):
                        P.op("pe", lambda e: e.matmul(pq[:, pr * 64:(pr + 1) * 64], lhsT=Qm[rows, pr, n, :],
                                                      rhs=Sbf[rows, pr, n, :], start=False, stop=(n == HN - 1)),
                             reads=[t_Qm, t_Sbf], writes=[tpq])
                    yield
                pqs.append((pq, tpq))

            def hv(t, e2):
                return t.rearrange("p (a e d) -> p a e d", e=2, d=64)[:, :, e2, :]
            for e2 in range(2):
                pq, tpq = pqs[e2]
                pq3 = pq[:, 0:128].rearrange("p (a d) -> p a d", d=64)
                P.op("act", lambda e: e.activation(out=hv(vsq[:, 1, :], e2), in_=pq3, func=AF.Square),
                     reads=[tpq], writes=[t_osq])
                P.op("dve", lambda e: e.tensor_reduce(out=ostat[:, 2 * e2:2 * e2 + 2], in_=hv(vsq[:, 1, :], e2), axis=AX.X, op=ALU.add),
                     reads=[t_osq], writes=[t_ostat])
                yield
            rstd_act(ostat[:, 4:8], ostat[:, 0:4], 64, t_ostat, t_ostat)
            yield
            for e2 in range(2):
                pq, tpq = pqs[e2]
                for pr in range(2):
                    h = 2 * pr + e2
                    P.op("dve", lambda e: e.scalar_tensor_tensor(
                        out=mixed[:, 768 + h * 64:768 + (h + 1) * 64], in0=pq[:, pr * 64:(pr + 1) * 64],
                        scalar=ostat[:, 4 + 2 * e2 + pr:5 + 2 * e2 + pr], in1=sg4[:, b, h * 64:(h + 1) * 64],
                        op0=ALU.mult, op1=ALU.mult),
                        reads=[tpq, t_ostat, t_sg4], writes=[t_mixed])
                yield

        def interleave(gens):
            ent = []
            for g in gens:
                if isinstance(g, tuple):
                    ent.append([g[0], g[1], g[2]])
                else:
                    ent.append([g, 1, 0])
            rnd = 0
            while ent:
                for e_ in list(ent):
                    if e_[2] > rnd:
                        continue
                    for _ in range(e_[1]):
                        try:
                            next(e_[0])
                        except StopIteration:
                            ent.remove(e_)
                            break
                rnd += 1

        def out_proj(b, wo):
            mixed = mixed2[:, b % 2, :]; t_mixed = t_mixed2[b % 2]
            pm_, tpm = bank()
            yield
            pmb = pm_[:].bitcast(BF16)
            for dc in range(8):
                P.op("pe", lambda e: e.transpose(out=pmb[:, dc * 128:(dc + 1) * 128],
                                                 in_=mixed[:, dc * 128:(dc + 1) * 128], identity=ident[:]),
                     reads=[t_mixed, t_ident], writes=[tpm])
            yield
            P.op("act", lambda e: e.activation(out=mixT[:].rearrange("p c t -> p (c t)"), in_=pmb, func=AF.Copy),
                 reads=[tpm], writes=[t_mixT])
            yield
            for hf in range(2):
                po2, tpo2 = bank()
                wsl, twsl = wo[hf]
                for dc in range(8):
                    P.op("pe", lambda e: e.matmul(po2[:], lhsT=mixT[:, dc, :], rhs=wsl[:, dc * 512:(dc + 1) * 512],
                                                  start=(dc == 0), stop=(dc == 7)),
                         reads=[t_mixT, twsl], writes=[tpo2])
                P.op("dve", lambda e: e.tensor_tensor(out=xres[:, b, hf * 512:(hf + 1) * 512], in0=po2[:],
                                                      in1=xres[:, b, hf * 512:(hf + 1) * 512], op=ALU.add),
                     reads=[tpo2, t_x[b]], writes=[t_x[b]])
                yield

        def ffn(l):
            units = [(g, cc, sub) for g in range(11) for cc in range(2) for sub in range(2)]
            pend = None
            gu = tgu = None

            def stage2(p_):
                c, sub, s_, pup, tpup = p_
                P.op("act", lambda e: e.activation(out=f2[:, s_, :], in_=f1[:, s_, :], func=AF.Silu),
                     reads=[t_f1[s_]], writes=[t_f2[s_]])
                P.op("dve", lambda e: e.tensor_tensor(out=aT[:, c, sub * 256:(sub + 1) * 256], in0=f2[:, s_, :],
                                                      in1=pup[:, 0:256], op=ALU.mult),
                     reads=[t_f2[s_], tpup], writes=[t_aT])

            for k, (g, cc, sub) in enumerate(units):
                if cc == 0 and sub == 0:
                    if g > 0:
                        ring_done(1)
                    ring_fill()
                    gu, tgu = ring_next()
                c = 2 * g + cc
                cw = 22 + 4 * c
                pgt, tpgt = bank()
                pup, tpup = bank()
                for dc in range(8):
                    P.op("pe", lambda e: e.matmul(pgt[:, 0:258], lhsT=gu[:, dc * 256 + cc * 128: dc * 256 + (cc + 1) * 128],
                                                  rhs=hT[:, dc, sub * 256: sub * 256 + 258],
                                                  start=(dc == 0), stop=(dc == 7)),
                         reads=[tgu, t_hT], writes=[tpgt])
                for dc in range(8):
                    P.op("pe", lambda e: e.matmul(pup[:, 0:256], lhsT=gu[:, 2048 + dc * 256 + cc * 128: 2048 + dc * 256 + (cc + 1) * 128],
                                                  rhs=hT[:, dc, 2 + sub * 256: 2 + sub * 256 + 256],
                                                  start=(dc == 0), stop=(dc == 7)),
                         reads=[tgu, t_hT], writes=[tpup])
                s_ = k % 2
                P.op("act", lambda e: e.activation(out=f1[:, s_, :], in_=pgt[:, 2:258], func=AF.Identity,
                                                   scale=cp(l, cw + 2, cw + 3), bias=cp(l, cw + 3, cw + 4)),
                     reads=[tpgt, t_colpar], writes=[t_f1[s_]])
                P.op("dve", lambda e: e.scalar_tensor_tensor(out=f2[:, s_, :], in0=pgt[:, 1:257],
                                                             scalar=cp(l, cw + 1, cw + 2), in1=f1[:, s_, :],
                                                             op0=ALU.mult, op1=ALU.add),
                     reads=[tpgt, t_colpar, t_f1[s_]], writes=[t_f2[s_]])
                P.op("dve", lambda e: e.scalar_tensor_tensor(out=f1[:, s_, :], in0=pgt[:, 0:256],
                                                             scalar=cp(l, cw, cw + 1), in1=f2[:, s_, :],
                                                             op0=ALU.mult, op1=ALU.add),
                     reads=[tpgt, t_colpar, t_f2[s_]], writes=[t_f1[s_]])
                if pend is not None:
                    stage2(pend)
                pend = (c, sub, s_, pup, tpup)
            stage2(pend)
            ring_done(1)
            ck(9)
            dbank = [[bank() for hf in range(2)] for b in range(NB)]
            for pc in range(6):
                ring_fill()
                wdp, twdp = ring_next()
                for cc in range(4):
                    c = pc * 4 + cc
                    if c >= NCH:
                        break
                    for b in range(NB):
                        for hf in range(2):
                            pbk, tpbk = dbank[b][hf]
                            P.op("pe", lambda e: e.matmul(pbk[:], lhsT=aT[:, c, b * 128:(b + 1) * 128],
                                                          rhs=wdp[:, cc * 1024 + hf * 512: cc * 1024 + (hf + 1) * 512],
                                                          start=(c == 0), stop=(c == NCH - 1)),
                                 reads=[t_aT, twdp], writes=[tpbk])
                ring_done(1)
            for b in range(NB):
                for hf in range(2):
                    pbk, tpbk = dbank[b][hf]
                    P.op("dve", lambda e: e.tensor_tensor(out=xres[:, b, hf * 512:(hf + 1) * 512], in0=pbk[:],
                                                          in1=xres[:, b, hf * 512:(hf + 1) * 512], op=ALU.add),
                         reads=[tpbk, t_x[b]], writes=[t_x[b]])

        def main_loop():
            for b in range(NB):
                P.dma("sp", xres[:, b, :], x_d[b * 128:(b + 1) * 128, :], writes=[t_x[b]])
            P.dma("sp", wfm[:], winfm_s[0], reads=[t_sc[("fm", 0)]], writes=[t_wfm])
            P.dma("sp", wtm[:], wintm_s[0], reads=[t_sc[("tm", 0)]], writes=[t_wtm])
            ring_fill()
            for ti in range(ntile):
                for b in range(NB if ti > 0 else 0):
                    P.dma("pool", xres[:, b, :],
                          x_d[ti * TT + b * 128: ti * TT + (b + 1) * 128, :], writes=[t_x[b]])
                for l in range(nl):
                    last = (ti == ntile - 1 and l == nl - 1)
                    rmsnorm_to_hT(l, 0)
                    ck(2)
                    fm_phase(l)
                    if not last:
                        P.dma("pool", wfm[:], winfm_s[(l + 1) % nl], reads=[t_sc[("fm", (l + 1) % nl)]], writes=[t_wfm])
                    ck(3)
                    tm_phase(l)
                    if not last:
                        P.dma("pool", wtm[:], wintm_s[(l + 1) % nl], reads=[t_sc[("tm", (l + 1) % nl)]], writes=[t_wtm])
                    ck(4)
                    P.op("pool", lambda e: e.tensor_tensor(
                        out=sg4[:], in0=sg4[:], in1=rowpar[:, l, 256:512].unsqueeze(1).to_broadcast([128, NB, 256]),
                        op=ALU.mult), reads=[t_sg4, t_rowpar], writes=[t_sg4])
                    rstd_act(vstat[:, 1, :], vstat[:, 0, :], 64, t_vstat, t_vstat)
                    ring_fill()
                    wo = [ring_next(), ring_next()]
                    interleave([chain_c1(l, 0)])
                    for b in range(NB):
                        gb = ti * NB + b
                        gens = [(chain_c2(l, b), 2, 0), chain_b(l, b, gb), (chain_a(l, b), 1, 3)]
                        if b + 1 < NB:
                            gens.append((chain_c1(l, b + 1), 1, 2))
                        if b > 0:
                            gens.append((out_proj(b - 1, wo), 1, 1))
                        interleave(gens)
                        ck(6)
                    interleave([out_proj(NB - 1, wo)])
                    ck(8)
                    ring_done(2)
                    P.op("pool", lambda e: e.tensor_copy(out=kT2[:, l, :, 0:128], in_=kT2[:, l, :, TT:TT + 128]),
                         reads=[t_kT2[l]], writes=[t_kT2[l]])
                    P.op("pool", lambda e: e.tensor_copy(out=vaug[:, l, 0, :, :], in_=vaug[:, l, NB, :, :]),
                         reads=[t_vaug[l]], writes=[t_vaug[l]])
                    rmsnorm_to_hT(l, 8)
                    P.op("pool", lambda e: e.tensor_copy(out=hT[:, :, 0:2], in_=halo[:, l, :, :]),
                         reads=[t_halo[l], t_hT], writes=[t_hT])
                    P.op("pool", lambda e: e.tensor_copy(out=halo[:, l, :, :], in_=hT[:, :, TT:TT + 2]),
                         reads=[t_hT, t_halo[l]], writes=[t_halo[l]])
                    ffn(l)
                t_o = T("out")
                for b in range(NB):
                    P.dma("pool", out_d[ti * TT + b * 128: ti * TT + (b + 1) * 128, :], xres[:, b, :], reads=[t_x[b]],
                          writes=[t_o], add=True)

        try:
            main_loop()
        except _Stop:
            pass
        P.drain("sp")
    build.stats = (P.nops, P.nwait, dict(P.cnt))
    build.marks = P.marks
    return nc


def _t5_bucket(dist):
    max_exact = 16
    n = np.maximum(dist, 0)
    is_small = n < max_exact
    nf = np.maximum(n, 1).astype(np.float32)
    large = max_exact + (np.log(nf / max_exact) / np.log(128 / max_exact) * (32 - max_exact)).astype(np.int32)
    large = np.minimum(large, 31)
    return np.where(is_small, n, large)


def prep_shared(inp, nl=NL):
    f = np.float32
    w_in = np.asarray(inp["w_in"], f)
    tm = list(range(0, 512)) + list(range(1792, 2304)) + list(range(1152, 1280))
    fm = (list(range(512, 1024)) + list(range(1024, 1088)) * 2 + list(range(1088, 1152)) * 2
          + list(range(1280, 1536)) + list(range(1536, 1792)))
    idx = np.array(tm + fm)
    sh = {}
    sh["w_in_r"] = np.ascontiguousarray(w_in[:nl][:, :, idx])
    sh["w_out"] = np.ascontiguousarray(np.asarray(inp["w_out"], f)[:nl])
    sh["w_gate"] = np.ascontiguousarray(np.asarray(inp["w_gate"], f)[:nl])
    sh["w_up"] = np.ascontiguousarray(np.asarray(inp["w_up"], f)[:nl])
    sh["w_down"] = np.ascontiguousarray(np.asarray(inp["w_down"], f)[:nl])
    colpar = np.zeros((128, nl, 110), f)
    lg = np.asarray(inp["hgrn_lb_logits"], f)
    for l in range(nl):
        colpar[:, l, 0:8] = np.asarray(inp["norm1_g"], f)[l].reshape(8, 128).T
        colpar[:, l, 8:16] = np.asarray(inp["norm2_g"], f)[l].reshape(8, 128).T
        colpar[:, l, 16] = np.tile(np.asarray(inp["q_norm_g"], f)[l], 2)
        colpar[:, l, 17] = np.tile(np.asarray(inp["k_norm_g"], f)[l], 2)
        colpar[:, l, 18:20] = lg[0].reshape(2, 128).T
        colpar[:, l, 20:22] = lg[min(1, lg.shape[0] - 1)].reshape(2, 128).T
        cw = np.asarray(inp["conv_w"], f)[l]
        cb = np.asarray(inp["conv_b"], f)[l]
        blk = np.stack([cw[0], cw[1], cw[2], cb], axis=-1).reshape(NCH, 128, 4)
        colpar[:, l, 22:110] = blk.transpose(1, 0, 2).reshape(128, 88)
    sh["colpar"] = colpar
    rowpar = np.zeros((nl, 520), f)
    for l in range(nl):
        rowpar[l, 0:256] = np.asarray(inp["gmlp_vnorm_g"], f)[l].reshape(256)
        rowpar[l, 256:512] = np.tile(np.asarray(inp["hgrn_onorm_g"], f)[l], 4)
        rowpar[l, 512:520] = np.asarray(inp["attn_sinks"], f)[l]
    sh["rowpar"] = rowpar
    ws = np.asarray(inp["gmlp_w_s"], f)[:nl]
    sh["wsT"] = np.ascontiguousarray(ws.transpose(0, 3, 1, 2))
    sh["bT"] = np.ascontiguousarray(np.asarray(inp["gmlp_b_s"], f)[:nl].transpose(0, 2, 1))
    rb = np.asarray(inp["rel_bias"], f)
    j = np.arange(128)[:, None]
    i = np.arange(128)[None, :]
    biasT = np.full((128, 2, 8, 128), -30000.0, f)
    d_prev = i + 128 - j
    d_cur = i - j
    for kb, dd in ((0, d_prev), (1, d_cur)):
        valid = (dd >= 0) & (dd < 128)
        g = rb[_t5_bucket(dd)]
        g = np.where(valid[:, :, None], g, np.float32(-30000.0))
        biasT[:, kb, :, :] = g.transpose(0, 2, 1)
    sh["biasT"] = biasT
    s_ = np.arange(128)[:, None]
    t_ = np.arange(128)[None, :]
    cst = np.zeros((128, 3, 128), f)
    cst[:, 0, :] = (s_ <= t_)
    cst[:, 1, :] = (s_ <= t_) & ((s_ // HCH) == (t_ // HCH))
    cst[:, 2, :] = np.broadcast_to((np.arange(128) % HCH != 0)[None, :], (128, 128))
    sh["cst"] = cst
    sh["cm"] = (np.arange(128)[:, None] // HCH == np.arange(HN)[None, :]).astype(f)
    b2 = np.zeros((128, 2), f)
    b2[0:64, 0] = 1
    b2[64:128, 1] = 1
    sh["blk2"] = b2
    sh["blk2T"] = np.ascontiguousarray(b2.T)
    return sh


_NC_CACHE = {}


def kernel(**inputs):
    x = np.asarray(inputs["x"], np.float32)
    B, S, _ = x.shape
    sh = prep_shared(inputs)
    key = (S, NL)
    if key not in _NC_CACHE:
        _NC_CACHE[key] = build(S, NL)
    nc = _NC_CACHE[key]
    in_maps = []
    for c in range(8):
        m = dict(sh)
        m["x"] = np.ascontiguousarray(x[c % B])
        in_maps.append(m)
    res = run_bass_kernel_spmd(nc, in_maps, core_ids=list(range(8)))
    out = np.stack([np.asarray(res.results[b]["out"], np.float32) for b in range(B)], axis=0)
    return out
```
